# Optimizing a Trainium2 kernel written in Bass

```python
import math
import jax, jax.numpy as jnp
from jax import lax
import numpy as np

D_MODEL = 1024
BATCH = 4
SEQ = 8192
DEPTH = 1

GRID_W = 64
CTX_LEN = 256
EPS = 1e-6
GLA_HEADS = 4
GLA_DK = D_MODEL // (2 * GLA_HEADS)
GLA_DV = D_MODEL // GLA_HEADS
GLA_WIDTH = GLA_HEADS * GLA_DV
GLA_RANK = 16
GLA_TAU = 16.0
GLA_CHUNK = 64
DIFF_DK = 64
DIFF_DV = 2 * DIFF_DK
DIFF_HEADS = D_MODEL // DIFF_DV
DIFF_WIDTH = DIFF_HEADS * DIFF_DV
Q_BLOCK = 128
ROPE_BASE = 10000.0
ROPE_AXIS_DIM = DIFF_DK // 2
IN_SIZES = (GLA_HEADS * GLA_DK, GLA_HEADS * GLA_DK, GLA_WIDTH, GLA_WIDTH, 2 * GLA_RANK,
            DIFF_HEADS * 2 * DIFF_DK, DIFF_HEADS * 2 * DIFF_DK, DIFF_WIDTH, DIFF_WIDTH,
            D_MODEL, D_MODEL)
IN_WIDTH = sum(IN_SIZES)

kernel_name = 'hybrid_gla_diffattn_gated_parallel_block'


def rms_norm(t):
    tf = t.astype(jnp.float32)
    return (tf * lax.rsqrt(jnp.mean(tf * tf, axis=-1, keepdims=True) + EPS)).astype(t.dtype)


def modulate(t, shift, scale):
    return rms_norm(t) * (1 + scale) + shift


def heads(t, n):
    b_, l_, w_ = t.shape
    return t.reshape(b_, l_, n, w_ // n).transpose(0, 2, 1, 3)


def merge_heads(t):
    b_, h_, l_, d_ = t.shape
    return t.transpose(0, 2, 1, 3).reshape(b_, l_, h_ * d_)


def rev(t):
    return jnp.flip(t, axis=2)


def project(h, w_in):
    p = h @ w_in
    return jnp.split(p, np.cumsum(IN_SIZES)[:-1].tolist(), axis=-1)


def axial_rope_tables(n_tokens, dtype):
    rows = n_tokens // GRID_W
    row = jnp.repeat(jnp.arange(rows, dtype=jnp.float32), GRID_W)
    col = jnp.tile(jnp.arange(GRID_W, dtype=jnp.float32), rows)
    inv_freq = ROPE_BASE ** (-jnp.arange(0, ROPE_AXIS_DIM, 2, dtype=jnp.float32) / ROPE_AXIS_DIM)
    ang_r = row[:, None] * inv_freq
    ang_c = col[:, None] * inv_freq
    two = lambda a: jnp.concatenate([a, a], axis=-1).astype(dtype)
    return (two(jnp.cos(ang_r)), two(jnp.sin(ang_r)), two(jnp.cos(ang_c)), two(jnp.sin(ang_c)))


def rope_axis(t, cos, sin):
    t1, t2 = jnp.split(t, 2, axis=-1)
    return t * cos + jnp.concatenate([-t2, t1], axis=-1) * sin


def apply_rope(t, tabs):
    cr, sr, cc, sc = tabs
    tr, tc = jnp.split(t, 2, axis=-1)
    return jnp.concatenate([rope_axis(tr, cr, sr), rope_axis(tc, cc, sc)], axis=-1)


def gla_streams(gq, gk, gv, g_lr, w_dec, b_dec):
    q = heads(gq, GLA_HEADS) * GLA_DK ** -0.5
    k = heads(gk, GLA_HEADS)
    v = heads(gv, GLA_HEADS)
    lr_f, lr_b = jnp.split(g_lr, 2, axis=-1)
    la_f = jax.nn.log_sigmoid((lr_f @ w_dec[0] + b_dec[0]).astype(jnp.float32)) / GLA_TAU
    la_b = jax.nn.log_sigmoid((lr_b @ w_dec[1] + b_dec[1]).astype(jnp.float32)) / GLA_TAU
    return q, k, v, heads(la_f, GLA_HEADS), heads(la_b, GLA_HEADS)


def gla_final_state(k, v, log_a):
    cum = jnp.cumsum(log_a.astype(jnp.float32), axis=2)
    k_dec = k.astype(jnp.float32) * jnp.exp(cum[:, :, -1:] - cum)
    return jnp.einsum('bhld,bhlv->bhdv', k_dec, v.astype(jnp.float32))


def gla_chunked(q, k, v, log_a, s0):
    b_, h_, l_, _ = q.shape
    dv = v.shape[-1]
    n = l_ // GLA_CHUNK

    def chunks(t):
        return t.astype(jnp.float32).reshape(b_, h_, n, GLA_CHUNK, t.shape[-1])

    q, k, v, log_a = chunks(q), chunks(k), chunks(v), chunks(log_a)
    cum = jnp.cumsum(log_a, axis=3)
    ref = cum[:, :, :, GLA_CHUNK // 2 - 1:GLA_CHUNK // 2]
    scores = jnp.einsum('bhnid,bhnjd->bhnij', q * jnp.exp(cum - ref), k * jnp.exp(ref - cum))
    tri = jnp.tril(jnp.ones((GLA_CHUNK, GLA_CHUNK), dtype=bool))
    o_intra = jnp.einsum('bhnij,bhnjv->bhniv', jnp.where(tri, scores, 0.0), v)
    last = cum[:, :, :, -1:]
    q_inter = q * jnp.exp(cum)
    k_state = k * jnp.exp(last - cum)
    decay = jnp.exp(last[:, :, :, 0])

    def step(s, inp):
        qc, kc, vc, dc = inp
        o = jnp.einsum('bhid,bhdv->bhiv', qc, s)
        s = dc[..., None] * s + jnp.einsum('bhid,bhiv->bhdv', kc, vc)
        return s, o

    front = lambda t: jnp.moveaxis(t, 2, 0)
    _, o_inter = lax.scan(step, s0.astype(jnp.float32),
                          (front(q_inter), front(k_state), front(v), front(decay)))
    o = o_intra + jnp.moveaxis(o_inter, 0, 2)
    return o.reshape(b_, h_, l_, dv)


def bidir_gla(q, k, v, la_f, la_b, s_f, s_b):
    fwd = gla_chunked(q, k, v, la_f, s_f)
    bwd = rev(gla_chunked(rev(q), rev(k), rev(v), rev(la_b), s_b))
    return (fwd + bwd).astype(v.dtype)


def diff_streams(dq, dk, dv, q_gain, k_gain):
    q1, q2 = jnp.split(heads(dq, DIFF_HEADS), 2, axis=-1)
    k1, k2 = jnp.split(heads(dk, DIFF_HEADS), 2, axis=-1)
    v = heads(dv, DIFF_HEADS)
    return (rms_norm(q1) * q_gain, rms_norm(q2) * q_gain,
            rms_norm(k1) * k_gain, rms_norm(k2) * k_gain, v)


def diff_lambda_value(lam_params, lam_init):
    lq1, lk1, lq2, lk2 = lam_params.astype(jnp.float32)
    return jnp.exp(jnp.sum(lq1 * lk1)) - jnp.exp(jnp.sum(lq2 * lk2)) + lam_init


def diff_attention(q1, q2, k1, k2, v, lam):
    b_, h_, lq, d = q1.shape
    nb = lq // Q_BLOCK
    scale = d ** -0.5

    def blocks(t):
        return jnp.moveaxis(t.reshape(b_, h_, nb, Q_BLOCK, d), 2, 0)

    def one_block(qs):
        qb1, qb2 = qs
        s1 = jnp.einsum('bhqd,bhkd->bhqk', qb1, k1).astype(jnp.float32) * scale
        s2 = jnp.einsum('bhqd,bhkd->bhqk', qb2, k2).astype(jnp.float32) * scale
        a = jax.nn.softmax(s1, axis=-1) - lam * jax.nn.softmax(s2, axis=-1)
        return jnp.einsum('bhqk,bhkv->bhqv', a.astype(v.dtype), v)

    o = lax.map(one_block, (blocks(q1), blocks(q2)))
    return jnp.moveaxis(o, 0, 2).reshape(b_, h_, lq, v.shape[-1])


def combine(o_gla, gla_gate, o_diff, diff_gate, m_gla, m_diff,
            gla_gain, diff_gain, lam_init, w_br_gla, w_br_diff, w_out):
    a = merge_heads(rms_norm(o_gla) * gla_gain) * jax.nn.silu(gla_gate)
    b = merge_heads(rms_norm(o_diff) * diff_gain * (1 - lam_init)) * jax.nn.silu(diff_gate)
    y = jax.nn.sigmoid(m_gla) * (a @ w_br_gla) + jax.nn.sigmoid(m_diff) * (b @ w_br_diff)
    return y @ w_out


def setup_inputs(seed: int = 0) -> dict:
    key = jax.random.key(seed)
    ks = jax.random.split(key, 17)
    nrm = lambda k, shape, s: jax.random.normal(k, shape, jnp.float32) * s
    return {
        'x': nrm(ks[0], (BATCH, SEQ, D_MODEL), 1.0),
        'c': nrm(ks[1], (BATCH, D_MODEL), 1.0),
        'ctx': nrm(ks[2], (BATCH, CTX_LEN, D_MODEL), 1.0),
        'c_ctx': nrm(ks[3], (D_MODEL,), 1.0),
        'w_ada': nrm(ks[4], (DEPTH, D_MODEL, 3 * D_MODEL), D_MODEL ** -0.5),
        'b_ada': nrm(ks[5], (DEPTH, 3 * D_MODEL), 0.02),
        'w_in': nrm(ks[6], (DEPTH, D_MODEL, IN_WIDTH), D_MODEL ** -0.5),
        'gla_w_decay': nrm(ks[7], (DEPTH, 2, GLA_RANK, GLA_HEADS * GLA_DK), GLA_RANK ** -0.5),
        'gla_b_decay': nrm(ks[8], (DEPTH, 2, GLA_HEADS * GLA_DK), 0.1),
        'gla_norm': 1.0 + nrm(ks[9], (DEPTH, GLA_DV), 0.02),
        'diff_q_norm': 1.0 + nrm(ks[10], (DEPTH, DIFF_DK), 0.02),
        'diff_k_norm': 1.0 + nrm(ks[11], (DEPTH, DIFF_DK), 0.02),
        'diff_lambda': nrm(ks[12], (DEPTH, 4, DIFF_DK), 0.1),
        'diff_norm': 1.0 + nrm(ks[13], (DEPTH, DIFF_DV), 0.02),
        'w_br_gla': nrm(ks[14], (DEPTH, GLA_WIDTH, D_MODEL), GLA_WIDTH ** -0.5),
        'w_br_diff': nrm(ks[15], (DEPTH, DIFF_WIDTH, D_MODEL), DIFF_WIDTH ** -0.5),
        'w_out': nrm(ks[16], (DEPTH, D_MODEL, D_MODEL), D_MODEL ** -0.5),
    }


def reference(x, c, ctx, c_ctx, w_ada, b_ada, w_in, gla_w_decay, gla_b_decay, gla_norm,
              diff_q_norm, diff_k_norm, diff_lambda, diff_norm, w_br_gla, w_br_diff, w_out):
    rope = axial_rope_tables(x.shape[1], x.dtype)
    for layer in range(DEPTH):
        lam_init = 0.8 - 0.6 * math.exp(-0.3 * layer)
        mod_x = jax.nn.silu(c) @ w_ada[layer] + b_ada[layer]
        mod_c = jax.nn.silu(c_ctx) @ w_ada[layer] + b_ada[layer]
        sh_x, sc_x, g_x = jnp.split(mod_x[:, None, :], 3, axis=-1)
        sh_c, sc_c, g_c = jnp.split(mod_c, 3, axis=-1)
        pl = project(modulate(x, sh_x, sc_x), w_in[layer])
        pc = project(modulate(ctx, sh_c, sc_c), w_in[layer])

        q, k, v, la_f, la_b = gla_streams(pl[0], pl[1], pl[2], pl[4],
                                          gla_w_decay[layer], gla_b_decay[layer])
        qc, kc, vc, la_fc, la_bc = gla_streams(pc[0], pc[1], pc[2], pc[4],
                                               gla_w_decay[layer], gla_b_decay[layer])
        s_f = gla_final_state(kc, vc, la_fc)
        s_b = gla_final_state(rev(kc), rev(vc), rev(la_bc))
        o_gla = bidir_gla(q, k, v, la_f, la_b, s_f, s_b)

        q1, q2, k1, k2, dv = diff_streams(pl[5], pl[6], pl[7], diff_q_norm[layer], diff_k_norm[layer])
        q1, q2, k1, k2 = (apply_rope(q1, rope), apply_rope(q2, rope),
                          apply_rope(k1, rope), apply_rope(k2, rope))
        cq1, cq2, ck1, ck2, cv = diff_streams(pc[5], pc[6], pc[7], diff_q_norm[layer], diff_k_norm[layer])
        lam = diff_lambda_value(diff_lambda[layer], lam_init)
        o_diff = diff_attention(q1, q2,
                                jnp.concatenate([ck1, k1], axis=2),
                                jnp.concatenate([ck2, k2], axis=2),
                                jnp.concatenate([cv, dv], axis=2), lam)

        out_x = combine(o_gla, pl[3], o_diff, pl[8], pl[9], pl[10], gla_norm[layer], diff_norm[layer],
                        lam_init, w_br_gla[layer], w_br_diff[layer], w_out[layer])
        if layer < DEPTH - 1:
            zero = jnp.zeros_like(s_f)
            o_gla_c = bidir_gla(qc, kc, vc, la_fc, la_bc, zero, zero)
            o_diff_c = diff_attention(cq1, cq2, ck1, ck2, cv, lam)
            out_c = combine(o_gla_c, pc[3], o_diff_c, pc[8], pc[9], pc[10], gla_norm[layer],
                            diff_norm[layer], lam_init, w_br_gla[layer], w_br_diff[layer], w_out[layer])
            ctx = ctx + g_c * out_c
        x = x + g_x * out_x
    return x
```

```python
import math
import numpy as np
import ml_dtypes
import concourse.bass as bass
import concourse.mybir as mybir
from concourse.bass_utils import run_bass_kernel_spmd

F32 = mybir.dt.float32
BF16 = mybir.dt.bfloat16
AF = mybir.ActivationFunctionType
ALU = mybir.AluOpType
AX = mybir.AxisListType
NPBF = ml_dtypes.bfloat16

D = 1024
SEQ = 8192
OWN = 4096
CTX = 256
NTOK = CTX + SEQ
EPS = 1e-6
IN_W = 9248
O_GQ, O_GK, O_GV, O_GG, O_LR, O_DQ, O_DK, O_DV, O_DG, O_MG, O_MD = (
    0, 512, 1024, 2048, 3072, 3104, 4128, 5152, 6176, 7200, 8224)
LAM_INIT = 0.8 - 0.6 * math.exp(-0.3 * 0)
SB_LO = 16640
SB_HI = 229344
NCB = 12
NCH = OWN // 128
LN_QS = -0.5 * math.log(128.0)


class Res:
    __slots__ = ("w", "r", "psum")

    def __init__(self, psum=False):
        self.w = None
        self.r = []
        self.psum = psum


class Eng:
    def __init__(self, nc, name, h, is_pe=False):
        self.nc = nc
        self.name = name
        self.h = h
        self.sem = nc.alloc_semaphore("s_" + name)
        self.count = 0
        self.seen = {}
        self.is_pe = is_pe
        self.dpool = None
        self.dn = 0

    def _wait(self, tok):
        sem, val = tok
        if sem is self.sem and self.is_pe:
            return
        key = id(sem)
        if self.seen.get(key, 0) >= val:
            return
        self.h.wait_ge(sem, val)
        self.seen[key] = val

    def _deps(self, reads, writes):
        for r in reads:
            if r.w is not None:
                self._wait(r.w)
            if r.psum:
                for t in r.r:
                    if t[0] is not self.sem:
                        self._wait(t)
        for w in writes:
            if w.w is not None:
                self._wait(w.w)
            for t in w.r:
                self._wait(t)

    def _mark(self, tok, reads, writes):
        for r in reads:
            if len(r.r) > 24:
                d = {}
                for s, v in r.r:
                    if d.get(id(s), (None, 0))[1] < v:
                        d[id(s)] = (s, v)
                r.r = list(d.values())
            r.r.append(tok)
        for w in writes:
            w.w = tok
            w.r = []

    def op(self, fn, reads=(), writes=(), inc=True):
        self._deps(reads, writes)
        ins = fn(self.h)
        if inc:
            self.count += 1
            ins.then_inc(self.sem, 1)
            tok = (self.sem, self.count)
        else:
            tok = (self.sem, self.count + 1)
        self._mark(tok, reads, writes)
        return ins

    def init_dma(self, npool):
        self.dpool = [[self.nc.alloc_semaphore("d_%s_%d" % (self.name, i)), 0] for i in range(npool)]

    def dma(self, out, in_, reads=(), writes=()):
        slot = self.dpool[self.dn % len(self.dpool)]
        self.dn += 1
        if slot[1] > 0:
            self._wait((slot[0], slot[1]))
        self._deps(reads, writes)
        slot[1] += 16
        ins = self.h.dma_start(out=out, in_=in_)
        ins.then_inc(slot[0], 16)
        self._mark((slot[0], slot[1]), reads, writes)
        return ins


class FW:
    def __init__(self, nc):
        self.nc = nc
        self.pe = Eng(nc, "pe", nc.tensor, is_pe=True)
        self.act = Eng(nc, "act", nc.scalar)
        self.dve = Eng(nc, "dve", nc.vector)
        self.pool = Eng(nc, "pool", nc.gpsimd)
        self.sp = Eng(nc, "sp", nc.sync)
        self.sp.init_dma(24)
        self.pool.init_dma(12)
        self.engs = [self.pe, self.act, self.dve, self.pool, self.sp]

    def barrier(self):
        toks = []
        for e in self.engs:
            if e.count > 0:
                toks.append((e.sem, e.count))
            if e.dpool:
                for s, c in e.dpool:
                    if c > 0:
                        toks.append((s, c))
        for e in self.engs:
            for t in toks:
                e._wait(t)


class SB:
    def __init__(self, nc):
        self.nc = nc
        self.off = SB_LO
        self.n = 0

    def mark(self):
        return self.off

    def reset(self, m):
        self.off = m

    def t(self, shape, dt):
        nb = int(np.prod(shape[1:])) * (4 if dt == F32 else 2)
        nb = (nb + 63) // 64 * 64
        assert self.off + nb <= SB_HI, ("SBUF overflow", self.off, nb)
        self.n += 1
        h = self.nc.alloc_sbuf_tensor_at("sb%d" % self.n, list(shape), dt, offset=self.off)
        self.off += nb
        return h


class Ring:
    def __init__(self, items, res=None, psum=False):
        if res is None:
            res = [Res(psum) for _ in items]
        self.items = list(zip(items, res))
        self.i = 0

    def next(self):
        it = self.items[self.i % len(self.items)]
        self.i += 1
        return it


def build_program(stop_after=99, dbg=False, NHEADS_RUN=8, NQG_RUN=8, cut=0):
    nc = bass.Bass("TRN2", target_bir_lowering=False)
    fw = FW(nc)
    pe, act, dve, pool, sp = fw.pe, fw.act, fw.dve, fw.pool, fw.sp
    sb = SB(nc)

    def din(name, shape, dt=F32):
        return nc.dram_tensor(name, list(shape), dt, kind="ExternalInput").ap()

    x_d = din("x", [SEQ, D])
    ctx_d = din("ctx", [CTX, D])
    cvec_d = din("cvec", [128, 8, 2])
    wada_d = din("w_ada", [D, 3 * D])
    bada_d = din("b_ada", [128, 24])
    badag_d = din("b_ada_g", [1, D])
    win_d = din("w_in", [D, IN_W])
    wdec_d = din("w_dec", [2, 17, 512])
    glan_d = din("gla_norm", [128, 2])
    qn_d = din("q_norm", [128, 1])
    kn_d = din("k_norm", [128, 1])
    dn_d = din("diff_norm", [128, 1])
    lam_d = din("lam", [1, 256])
    qkrow_d = din("qk_row", [1, 128])
    wbg_d = din("w_br_gla", [D, D])
    wbd_d = din("w_br_diff", [D, D])
    wo_d = din("w_out", [D, D])
    cos_d = din("cos_t", [128, NTOK])
    sin_d = din("sin_t", [128, NTOK])
    cb_d = din("cbf", [128, NCB, 128], BF16)
    out_d = nc.dram_tensor("out", [OWN, D], F32, kind="ExternalOutput").ap()

    hT_s = nc.dram_tensor("hT_s", [8, 128, NTOK], BF16).ap()
    r_hT = Res()
    dbg_out = {}

    def dout(name, shape, dt=F32):
        t = nc.dram_tensor(name, list(shape), dt, kind="ExternalOutput").ap()
        dbg_out[name] = t
        return t

    pairs = [nc.alloc_psum_tensor("pp%d" % i, [128, 2, 512], F32) for i in range(4)]
    banks = [pairs[i // 2][:, i % 2, :] for i in range(8)]

    cb = sb.t([128, NCB, 128], BF16)
    r_cb = Res()
    sp.dma(cb[:], cb_d, writes=[r_cb])
    C_ID, C_PERM, C_BLK64, C_ONES, C_M1F, C_M1B, C_TRIF, C_TRIB, C_RDF, C_RDB, C_MSKF, C_MSKB = range(12)
    modT = sb.t([128, 24, 2], F32)
    r_mod = Res()
    gxb = sb.t([128, D], F32)
    r_gxb = Res()
    smallv = sb.t([128, 8], F32)
    r_small = Res()
    sp.dma(smallv[:, 0:2], glan_d, writes=[r_small])
    sp.dma(smallv[:, 2:3], qn_d, writes=[r_small])
    sp.dma(smallv[:, 3:4], kn_d, writes=[r_small])
    sp.dma(smallv[:, 4:5], dn_d, writes=[r_small])
    base_mark = sb.mark()

    GW = O_LR + 32
    wg = sb.t([128, 8, GW], BF16)
    r_wg = Res()
    wdec = sb.t([17, 2, 512], BF16)
    r_wdec32, r_wdec = Res(), Res()
    p1_mark = sb.mark()
    wgst_r = Ring([sb.t([128, 8, 512], F32) for _ in range(2)])
    wdec32 = sb.t([17, 2, 512], F32)
    sp.dma(wdec32[:], wdec_d.rearrange("d k c -> k d c"), writes=[r_wdec32])
    pool.op(lambda e: e.tensor_copy(out=wdec[:], in_=wdec32[:]), reads=[r_wdec32], writes=[r_wdec])

    def load_wg_block(blk):
        c0 = blk * 512
        n = min(512, GW - c0)
        st, r_st = wgst_r.next()
        sp.dma(st[:, :, 0:n], win_d[:, c0:c0 + n].rearrange("(j p) c -> p j c", p=128), writes=[r_st])
        pool.op(lambda e: e.tensor_copy(out=wg[:, :, c0:c0 + n], in_=st[:, :, 0:n]), reads=[r_st], writes=[r_wg])

    cs32 = sb.t([128, 8, 2], F32)
    csb = sb.t([128, 8, 2], BF16)
    scb = sb.t([128, 8, 128], BF16)
    bada = sb.t([128, 24], F32)
    ones32 = sb.t([128, 128], F32)
    r_cs, r_csb, r_scb, r_bada, r_ones32 = Res(), Res(), Res(), Res(), Res()
    sp.dma(cs32[:], cvec_d, writes=[r_cs])
    sp.dma(bada[:], bada_d, writes=[r_bada])
    sp.dma(gxb[:], bass.AP(badag_d.tensor, 0, [[0, 128], [1, D]]), writes=[r_gxb])
    act.op(lambda e: e.activation(out=cs32[:], in_=cs32[:], func=AF.Silu), reads=[r_cs], writes=[r_cs])
    dve.op(lambda e: e.tensor_copy(out=csb[:], in_=cs32[:]), reads=[r_cs], writes=[r_csb])
    dve.op(lambda e: e.memset(ones32[:], 1.0), writes=[r_ones32])
    for j in range(8):
        dve.op(lambda e, j=j: e.tensor_scalar(out=scb[:, j, :], in0=ones32[:], scalar1=cs32[:, j, 0:1],
                                              scalar2=None, op0=ALU.mult),
               reads=[r_ones32, r_cs], writes=[r_scb])
    wst = [sb.t([128, 8, 512], F32) for _ in range(2)]
    wbf = [sb.t([128, 8, 512], BF16) for _ in range(2)]
    wst_r = Ring(wst)
    wbf_r = Ring(wbf)
    mod_ps = banks[0]
    r_modps = Res(True)
    g_ps = [banks[1], banks[2]]
    r_gps = [Res(True), Res(True)]
    for blk in range(6):
        st, r_st = wst_r.next()
        wb, r_wb = wbf_r.next()
        sp.dma(st[:], wada_d[:, blk * 512:(blk + 1) * 512].rearrange("(j p) c -> p j c", p=128), writes=[r_st])
        (dve if blk % 2 == 0 else pool).op(lambda e: e.tensor_copy(out=wb[:], in_=st[:]), reads=[r_st], writes=[r_wb])
        for q in range(4):
            m = blk * 4 + q
            for j in range(8):
                pe.op(lambda e, j=j, q=q, m=m: e.matmul(
                    mod_ps[:, 2 * m:2 * m + 2], lhsT=wb[:, j, q * 128:(q + 1) * 128], rhs=csb[:, j, :],
                    start=(j == 0), stop=(j == 7)),
                    reads=[r_wb, r_csb], writes=[r_modps], inc=(j == 7))
        if blk >= 4:
            gp, r_gp = g_ps[blk - 4], r_gps[blk - 4]
            for j in range(8):
                pe.op(lambda e, j=j: e.matmul(gp[:, :], lhsT=scb[:, j, :], rhs=wb[:, j, :],
                                              start=(j == 0), stop=(j == 7)),
                      reads=[r_wb, r_scb], writes=[r_gp], inc=(j == 7))
    mps3 = mod_ps[:, 0:48].rearrange("p (m t) -> p m t", t=2)
    for t in range(2):
        dve.op(lambda e, t=t: e.tensor_tensor(out=modT[:, :, t], in0=mps3[:, :, t], in1=bada[:, :], op=ALU.add),
               reads=[r_modps, r_bada], writes=[r_mod])
    dve.op(lambda e: e.tensor_scalar(out=modT[:, 8:16, :], in0=modT[:, 8:16, :], scalar1=1.0, scalar2=None,
                                     op0=ALU.add), reads=[r_mod], writes=[r_mod])
    for i in range(2):
        dve.op(lambda e, i=i: e.tensor_tensor(out=gxb[:, i * 512:(i + 1) * 512], in0=g_ps[i][:, :],
                                              in1=gxb[:, i * 512:(i + 1) * 512], op=ALU.add),
               reads=[r_gps[i], r_gxb], writes=[r_gxb])

    xt_r = Ring([sb.t([128, D], F32) for _ in range(3)])
    xn_r = Ring([sb.t([128, D], BF16) for _ in range(2)])
    junk = sb.t([128, D], BF16)
    r_junk = Res()
    st_r = Ring([sb.t([128, 4], F32) for _ in range(4)])
    hg_r = Ring([sb.t([128, 8, 512], BF16) for _ in range(2)])
    tp_banks = [banks[3], banks[4], banks[5]]
    tp_r = Ring([b.bitcast(BF16) for b in tp_banks], psum=True)
    groups = [("ctx", 0, 256)] + [("x", g * 512, 512) for g in range(SEQ // 512)]
    tiles = []
    tok0 = 0
    for (src, r0, n) in groups:
        for i in range(n // 128):
            tiles.append((src, r0, n, i, tok0))
        tok0 += n
    hg_cur = [None, None]

    def stage_a(ti):
        src, r0, n, i, tk = tiles[ti]
        xt, r_xt = xt_r.next()
        srcap = (ctx_d if src == "ctx" else x_d)[r0 + i * 128:r0 + (i + 1) * 128, :]
        sp.dma(xt[:], srcap, writes=[r_xt])
        stt, r_stt = st_r.next()
        act.op(lambda e: e.activation(out=junk[:], in_=xt[:], func=AF.Square, accum_out=stt[:, 0:1]),
               reads=[r_xt], writes=[r_junk, r_stt])
        act.op(lambda e: e.activation(out=stt[:, 1:2], in_=stt[:, 0:1], func=AF.Sqrt, bias=EPS, scale=1.0 / D),
               reads=[r_stt], writes=[r_stt])
        dve.op(lambda e: e.reciprocal(out=stt[:, 2:3], in_=stt[:, 1:2]), reads=[r_stt], writes=[r_stt])
        xn, r_xn = xn_r.next()
        dve.op(lambda e: e.tensor_scalar(out=xn[:], in0=xt[:], scalar1=stt[:, 2:3], scalar2=None, op0=ALU.mult),
               reads=[r_xt, r_stt], writes=[r_xn])
        tp, r_tp = tp_r.next()
        for j in range(8):
            pe.op(lambda e, j=j: e.transpose(tp[:, j * 128:(j + 1) * 128], xn[:, j * 128:(j + 1) * 128],
                                             cb[:, C_ID, :]),
                  reads=[r_xn, r_cb], writes=[r_tp], inc=(j == 7))
        return tp, r_tp

    def stage_b(ti, tp, r_tp):
        src, r0, n, i, tk = tiles[ti]
        if i == 0:
            hg_cur[0], hg_cur[1] = hg_r.next()
        hg, r_hg = hg_cur
        mcol = 1 if src == "ctx" else 0
        for j in range(8):
            if j % 2 == 0:
                act.op(lambda e, j=j: e.activation(out=hg[:, j, i * 128:(i + 1) * 128],
                                                   in_=tp[:, j * 128:(j + 1) * 128], func=AF.Identity,
                                                   bias=modT[:, j, mcol:mcol + 1],
                                                   scale=modT[:, 8 + j, mcol:mcol + 1]),
                       reads=[r_tp, r_mod], writes=[r_hg])
            else:
                dve.op(lambda e, j=j: e.tensor_scalar(out=hg[:, j, i * 128:(i + 1) * 128],
                                                      in0=tp[:, j * 128:(j + 1) * 128],
                                                      scalar1=modT[:, 8 + j, mcol:mcol + 1],
                                                      scalar2=modT[:, j, mcol:mcol + 1],
                                                      op0=ALU.mult, op1=ALU.add),
                       reads=[r_tp, r_mod], writes=[r_hg])
        if i == n // 128 - 1:
            pool.dma(hT_s[:, :, tk:tk + n].rearrange("j p t -> p j t"), hg[:, :, 0:n], reads=[r_hg], writes=[r_hT])

    NWB = (GW + 511) // 512
    pend = stage_a(0)
    for ti in range(len(tiles)):
        nxt = stage_a(ti + 1) if ti + 1 < len(tiles) else None
        stage_b(ti, *pend)
        pend = nxt
        if ti % 8 == 4 and ti // 8 < NWB:
            load_wg_block(ti // 8)
    fw.barrier()
    sb.reset(p1_mark)
    if dbg:
        d_mod = dout("d_mod", [128, 48])
        d_gxb = dout("d_gxb", [128, D])
        pool.dma(d_mod, modT[:].rearrange("p m t -> p (m t)"), reads=[r_mod])
        pool.dma(d_gxb, gxb[:], reads=[r_gxb])
        d_hT = dout("d_hT", [8, 128, NTOK], BF16)
        pool.dma(d_hT, hT_s, reads=[r_hT])
    def finish_zero():
        z = sb.t([128, D], F32)
        r_z = Res()
        r_out = Res()
        dve.op(lambda e: e.memset(z[:], 0.0), writes=[r_z])
        for i in range(OWN // 128):
            pool.dma(out_d[i * 128:(i + 1) * 128, :], z[:], reads=[r_z], writes=[r_out])
        fw.barrier()
        return nc, dbg_out

    if stop_after <= 0:
        return finish_zero()

    Sf_s = nc.dram_tensor("Sf_s", [NCH, 128, 1024], BF16).ap()
    r_Sf = [Res() for _ in range(NCH)]
    aT_s = nc.dram_tensor("aT_s", [8, 128, OWN], BF16).ap()
    r_aT = Res()
    pring = Ring(banks[0:6], psum=True)
    o_ps = [banks[6], banks[7]]
    r_ops = Res(True)
    hg_r = Ring([sb.t([128, 8, 512], BF16) for _ in range(2)])
    lrT_r = Ring([sb.t([32, 512], BF16) for _ in range(2)])
    for (t_, r_) in lrT_r.items:
        dve.op(lambda e: e.memset(t_[:], 1.0), writes=[r_])
    l_r = [Ring([sb.t([128, 4, 512], BF16) for _ in range(2)]) for _ in range(2)]
    etmp_r = Ring([sb.t([128, 512], F32) for _ in range(2)])
    ktm_r = Ring([sb.t([128, 4, 512], BF16) for _ in range(2)])
    vtm_r = Ring([sb.t([128, 4, 1024], BF16) for _ in range(2)])
    qT_r = Ring([sb.t([128, 4, 512], BF16) for _ in range(1)])
    kT_r = Ring([sb.t([128, 4, 512], BF16) for _ in range(1)])
    sg_r = Ring([sb.t([128, 8, 512], BF16) for _ in range(1)])
    eks_r = Ring([sb.t([128, 512], BF16) for _ in range(1)])
    ks_r = Ring([sb.t([128, 512], BF16) for _ in range(8)])
    dec_r = Ring([sb.t([128, 4], F32) for _ in range(8)])
    e_r = Ring([sb.t([128, 512], BF16) for _ in range(3)])
    qk_r = Ring([sb.t([128, 4, 128], BF16) for _ in range(2)])
    qi_r = Ring([sb.t([128, 4, 128], BF16) for _ in range(8)])
    AT_r = Ring([sb.t([128, 4, 128], BF16) for _ in range(8)])
    S32 = [sb.t([128, 4, 256], F32) for _ in range(2)]
    r_S32 = [Res(), Res()]
    Sb_bf = sb.t([128, 1024], BF16)
    r_Sbbf = Res()
    Sst_r = Ring([sb.t([128, 1024], BF16) for _ in range(2)])
    Sld_r = Ring([sb.t([128, 1024], BF16) for _ in range(2)])
    sq_r = Ring([sb.t([128, 8, 128], BF16) for _ in range(2)])
    rstd_r = Ring([sb.t([128, 8, 128], F32) for _ in range(1)])
    on_r = Ring([sb.t([128, 8, 128], F32) for _ in range(2)])
    aT_r = Ring([sb.t([128, 8, 128], BF16) for _ in range(2)])
    for d_ in range(2):
        dve.op(lambda e: e.memset(S32[d_][:], 0.0), writes=[r_S32[d_]])
    dve.op(lambda e: e.memset(Sb_bf[:], 0.0), writes=[r_Sbbf])
    cp_flip = [0]

    def evac(out_ap, in_ap, reads, writes):
        cp_flip[0] ^= 1
        if cp_flip[0]:
            act.op(lambda e: e.activation(out=out_ap, in_=in_ap, func=AF.Copy), reads=reads, writes=writes)
        else:
            dve.op(lambda e: e.tensor_copy(out=out_ap, in_=in_ap), reads=reads, writes=writes)

    def proj_fm(ps_ap, r_ps, c0, M, hg, r_hg, n):
        for j in range(8):
            pe.op(lambda e: e.matmul(ps_ap, lhsT=wg[:, j, c0:c0 + M], rhs=hg[:, j, 0:n],
                                     start=(j == 0), stop=(j == 7)),
                  reads=[r_wg, r_hg], writes=[r_ps], inc=(j == 7))

    def proj_tm(ps_ap, r_ps, c0, N, hg, r_hg, i):
        for j in range(8):
            pe.op(lambda e: e.matmul(ps_ap, lhsT=hg[:, j, i * 128:(i + 1) * 128], rhs=wg[:, j, c0:c0 + N],
                                     start=(j == 0), stop=(j == 7)),
                  reads=[r_wg, r_hg], writes=[r_ps], inc=(j == 7))

    def gla_group(tok0, nt, dirs, full):
        n = nt * 128
        hg, r_hg = hg_r.next()
        sp.dma(hg[:, :, 0:n], hT_s[:, :, tok0:tok0 + n].rearrange("j p t -> p j t"), reads=[r_hT], writes=[r_hg])
        G = {}
        for d_ in dirs:
            ps, r_ps = pring.next()
            proj_fm(ps[0:16, 0:n], r_ps, O_LR + 16 * d_, 16, hg, r_hg, n)
            lrT, r_lrT = lrT_r.next()
            evac(lrT[0:16, 0:n], ps[0:16, 0:n], [r_ps], [r_lrT])
            lt, r_lt = l_r[d_].next()
            for i in range(nt):
                ps, r_ps = pring.next()
                pe.op(lambda e: e.matmul(ps[:, :], lhsT=lrT[0:17, i * 128:(i + 1) * 128], rhs=wdec[0:17, d_, :],
                                         start=True, stop=True), reads=[r_lrT, r_wdec], writes=[r_ps])
                et, r_et = etmp_r.next()
                act.op(lambda e: e.activation(out=et[:], in_=ps[:, :], func=AF.Exp, scale=-1.0),
                       reads=[r_ps], writes=[r_et])
                act.op(lambda e: e.activation(out=lt[:, i, :], in_=et[:], func=AF.Ln, bias=1.0),
                       reads=[r_et], writes=[r_lt])
            G["l%d" % d_] = (lt, r_lt)
        ktm, r_ktm = ktm_r.next()
        vtm, r_vtm = vtm_r.next()
        for i in range(nt):
            ps, r_ps = pring.next()
            proj_tm(ps[:, :], r_ps, O_GK, 512, hg, r_hg, i)
            evac(ktm[:, i, :], ps[:, :], [r_ps], [r_ktm])
            for u in range(2):
                ps, r_ps = pring.next()
                proj_tm(ps[:, :], r_ps, O_GV + 512 * u, 512, hg, r_hg, i)
                evac(vtm[:, i, u * 512:(u + 1) * 512], ps[:, :], [r_ps], [r_vtm])
        G["k"] = (ktm, r_ktm)
        G["v"] = (vtm, r_vtm)
        if full:
            qT, r_qT = qT_r.next()
            kT, r_kT = kT_r.next()
            for h in range(4):
                ps, r_ps = pring.next()
                proj_fm(ps[:, 0:n], r_ps, O_GQ + 128 * h, 128, hg, r_hg, n)
                evac(qT[:, h, 0:n], ps[:, 0:n], [r_ps], [r_qT])
                ps, r_ps = pring.next()
                proj_fm(ps[:, 0:n], r_ps, O_GK + 128 * h, 128, hg, r_hg, n)
                evac(kT[:, h, 0:n], ps[:, 0:n], [r_ps], [r_kT])
            sg, r_sg = sg_r.next()
            for m in range(8):
                ps, r_ps = pring.next()
                proj_fm(ps[:, 0:n], r_ps, O_GG + 128 * m, 128, hg, r_hg, n)
                et, r_et = etmp_r.next()
                act.op(lambda e: e.activation(out=et[:, 0:n], in_=ps[:, 0:n], func=AF.Silu),
                       reads=[r_ps], writes=[r_et])
                dve.op(lambda e: e.tensor_scalar(out=sg[:, m, 0:n], in0=et[:, 0:n], scalar1=smallv[:, (m % 2):(m % 2) + 1],
                                                 scalar2=None, op0=ALU.mult),
                       reads=[r_et, r_small], writes=[r_sg])
            G["qT"] = (qT, r_qT)
            G["kT"] = (kT, r_kT)
            G["sg"] = (sg, r_sg)
        return G

    def state_pieces(d_, lt, r_lt, ktm, r_ktm, i):
        ps, r_ps = pring.next()
        pe.op(lambda e: e.matmul(ps[:, :], lhsT=cb[:, C_M1F + d_, :], rhs=lt[:, i, :], start=True, stop=True),
              reads=[r_cb, r_lt], writes=[r_ps])
        eks, r_eks = eks_r.next()
        act.op(lambda e: e.activation(out=eks[:], in_=ps[:, :], func=AF.Exp), reads=[r_ps], writes=[r_eks])
        ks, r_ks = ks_r.next()
        dve.op(lambda e: e.tensor_tensor(out=ks[:], in0=ktm[:, i, :], in1=eks[:], op=ALU.mult),
               reads=[r_ktm, r_eks], writes=[r_ks])
        ps, r_ps = pring.next()
        for h in range(4):
            pe.op(lambda e: e.matmul(ps[:, h:h + 1], lhsT=lt[:, i, h * 128:(h + 1) * 128], rhs=cb[:, C_TRIF, 127:128],
                                     start=True, stop=True), reads=[r_cb, r_lt], writes=[r_ps], inc=(h == 3))
        dec, r_dec = dec_r.next()
        act.op(lambda e: e.activation(out=dec[:], in_=ps[:, 0:4], func=AF.Exp), reads=[r_ps], writes=[r_dec])
        return ks, r_ks, dec, r_dec

    def state_update(d_, ks, r_ks, dec, r_dec, vtm, r_vtm, i):
        S, r_S = S32[d_], r_S32[d_]
        for hp in range(2):
            ps, r_ps = pring.next()
            for hh in range(2):
                h = hp * 2 + hh
                pe.op(lambda e: e.matmul(ps[:, hh * 256:(hh + 1) * 256], lhsT=ks[:, h * 128:(h + 1) * 128],
                                         rhs=vtm[:, i, h * 256:(h + 1) * 256], start=True, stop=True),
                      reads=[r_ks, r_vtm], writes=[r_ps], inc=(hh == 1))
            for hh in range(2):
                h = hp * 2 + hh
                dve.op(lambda e: e.scalar_tensor_tensor(out=S[:, h, :], in0=S[:, h, :], scalar=dec[:, h:h + 1],
                                                        in1=ps[:, hh * 256:(hh + 1) * 256], op0=ALU.mult, op1=ALU.add),
                       reads=[r_S, r_dec, r_ps], writes=[r_S])

    G = gla_group(0, 2, (0, 1), False)
    for i in (0, 1):
        ks, r_ks, dec, r_dec = state_pieces(0, *G["l0"], *G["k"], i)
        state_update(0, ks, r_ks, dec, r_dec, *G["v"], i)
    for i in (1, 0):
        ks, r_ks, dec, r_dec = state_pieces(1, *G["l1"], *G["k"], i)
        state_update(1, ks, r_ks, dec, r_dec, *G["v"], i)

    def prep_state(tok0, d_, order):
        G_ = gla_group(tok0, 4, (d_,), False)
        pcs = {i: state_pieces(d_, *G_["l%d" % d_], *G_["k"], i) for i in order}
        return G_, pcs

    jobs = [(CTX + g * 512, 0, (0, 1, 2, 3), g) for g in range(8)]
    jobs += [(CTX + g * 512, 1, (3, 2, 1, 0), None) for g in range(15, 7, -1)]
    nxt = prep_state(*jobs[0][0:3])
    for ji, (tok0_, d_, order, gown) in enumerate(jobs):
        G, pcs = nxt
        nxt = prep_state(*jobs[ji + 1][0:3]) if ji + 1 < len(jobs) else None
        for i in order:
            if gown is not None:
                c = gown * 4 + i
                sst, r_sst = Sst_r.next()
                act.op(lambda e: e.activation(out=sst[:], in_=S32[0][:].rearrange("p h v -> p (h v)"), func=AF.Copy),
                       reads=[r_S32[0]], writes=[r_sst])
                pool.dma(Sf_s[c], sst[:], reads=[r_sst], writes=[r_Sf[c]])
            ks, r_ks, dec, r_dec = pcs[i]
            state_update(d_, ks, r_ks, dec, r_dec, *G["v"], i)
    act.op(lambda e: e.activation(out=Sb_bf[:], in_=S32[1][:].rearrange("p h v -> p (h v)"), func=AF.Copy),
           reads=[r_S32[1]], writes=[r_Sbbf])
    for g in range(7, -1, -1):
        G = gla_group(CTX + g * 512, 4, (0, 1), True)
        qT, r_qT = G["qT"]
        kT, r_kT = G["kT"]
        sg, r_sg = G["sg"]
        vtm, r_vtm = G["v"]
        prepd = {}
        for i in range(3, -1, -1):
            sl = slice(i * 128, (i + 1) * 128)
            per = []
            for d_ in range(2):
                lt, r_lt = G["l%d" % d_]
                psA, r_psA = pring.next()
                psB, r_psB = pring.next()
                for h in range(4):
                    pe.op(lambda e: e.matmul(psA[:, h * 128:(h + 1) * 128], lhsT=lt[:, i, h * 128:(h + 1) * 128],
                                             rhs=cb[:, C_RDF + d_, :], start=True, stop=True),
                          reads=[r_lt, r_cb], writes=[r_psA], inc=(h == 3))
                for h in range(4):
                    pe.op(lambda e: e.matmul(psB[:, h * 128:(h + 1) * 128], lhsT=lt[:, i, h * 128:(h + 1) * 128],
                                             rhs=cb[:, C_TRIF + d_, :], start=True, stop=True),
                          reads=[r_lt, r_cb], writes=[r_psB], inc=(h == 3))
                e1, r_e1 = e_r.next()
                e2, r_e2 = e_r.next()
                e3, r_e3 = e_r.next()
                act.op(lambda e: e.activation(out=e1[:], in_=psA[:, :], func=AF.Exp, scale=-1.0, bias=LN_QS),
                       reads=[r_psA], writes=[r_e1])
                act.op(lambda e: e.activation(out=e2[:], in_=psA[:, :], func=AF.Exp), reads=[r_psA], writes=[r_e2])
                act.op(lambda e: e.activation(out=e3[:], in_=psB[:, :], func=AF.Exp, bias=LN_QS),
                       reads=[r_psB], writes=[r_e3])
                qs, r_qs = qk_r.next()
                kk, r_kk = qk_r.next()
                qi, r_qi = qi_r.next()
                dve.op(lambda e: e.tensor_tensor(out=qs[:], in0=qT[:, :, sl],
                                                 in1=e1[:].rearrange("p (h t) -> p h t", h=4), op=ALU.mult),
                       reads=[r_qT, r_e1], writes=[r_qs])
                dve.op(lambda e: e.tensor_tensor(out=kk[:], in0=kT[:, :, sl],
                                                 in1=e2[:].rearrange("p (h t) -> p h t", h=4), op=ALU.mult),
                       reads=[r_kT, r_e2], writes=[r_kk])
                pool.op(lambda e: e.tensor_tensor(out=qi[:], in0=qT[:, :, sl],
                                                  in1=e3[:].rearrange("p (h t) -> p h t", h=4), op=ALU.mult),
                        reads=[r_qT, r_e3], writes=[r_qi])
                psS, r_psS = pring.next()
                for h in range(4):
                    pe.op(lambda e: e.matmul(psS[:, h * 128:(h + 1) * 128], lhsT=kk[:, h, :], rhs=qs[:, h, :],
                                             start=True, stop=True),
                          reads=[r_kk, r_qs], writes=[r_psS], inc=(h == 3))
                AT, r_AT = AT_r.next()
                for h in range(4):
                    dve.op(lambda e: e.tensor_tensor(out=AT[:, h, :], in0=psS[:, h * 128:(h + 1) * 128],
                                                     in1=cb[:, C_MSKF + d_, :], op=ALU.mult),
                           reads=[r_psS, r_cb], writes=[r_AT])
                per.append((AT, r_AT, qi, r_qi))
            prepd[i] = (per, state_pieces(1, *G["l1"], *G["k"], i))
        for i in range(3, -1, -1):
            c = g * 4 + i
            sl = slice(i * 128, (i + 1) * 128)
            sld, r_sld = Sld_r.next()
            sp.dma(sld[:], Sf_s[c], reads=[r_Sf[c]], writes=[r_sld])
            per, (ks, r_ks, dec, r_dec) = prepd[i]
            (ATf, r_ATf, qif, r_qif), (ATb, r_ATb, qib, r_qib) = per
            for h in range(4):
                for u in range(2):
                    m = h * 2 + u
                    o_ap = o_ps[m // 4][:, (m % 4) * 128:(m % 4 + 1) * 128]
                    vs = slice(h * 256 + u * 128, h * 256 + u * 128 + 128)
                    pe.op(lambda e: e.matmul(o_ap, lhsT=vtm[:, i, vs], rhs=ATf[:, h, :], start=True, stop=False),
                          reads=[r_vtm, r_ATf], writes=[r_ops], inc=False)
                    pe.op(lambda e: e.matmul(o_ap, lhsT=vtm[:, i, vs], rhs=ATb[:, h, :], start=False, stop=False),
                          reads=[r_vtm, r_ATb], writes=[r_ops], inc=False)
                    pe.op(lambda e: e.matmul(o_ap, lhsT=sld[:, vs], rhs=qif[:, h, :], start=False, stop=False),
                          reads=[r_sld, r_qif], writes=[r_ops], inc=False)
                    pe.op(lambda e: e.matmul(o_ap, lhsT=Sb_bf[:, vs], rhs=qib[:, h, :], start=False, stop=True),
                          reads=[r_Sbbf, r_qib], writes=[r_ops], inc=(m == 7))
            state_update(1, ks, r_ks, dec, r_dec, vtm, r_vtm, i)
            act.op(lambda e: e.activation(out=Sb_bf[:], in_=S32[1][:].rearrange("p h v -> p (h v)"), func=AF.Copy),
                   reads=[r_S32[1]], writes=[r_Sbbf])
            on, r_on = on_r.next()
            sq, r_sq = sq_r.next()
            for bnk in range(2):
                dve.op(lambda e: e.tensor_copy(out=on[:, bnk * 4:(bnk + 1) * 4, :].rearrange("p m t -> p (m t)"),
                                               in_=o_ps[bnk][:, :]), reads=[r_ops], writes=[r_on])
            act.op(lambda e: e.activation(out=sq[:], in_=on[:], func=AF.Square), reads=[r_on], writes=[r_sq])
            ss = [pring.next(), pring.next()]
            for h in range(4):
                for u2 in range(2):
                    m = h * 2 + u2
                    ps, r_ps = ss[m // 4]
                    for u in range(2):
                        pe.op(lambda e: e.matmul(ps[:, (m % 4) * 128:(m % 4 + 1) * 128], lhsT=cb[:, C_ONES, :],
                                                 rhs=sq[:, h * 2 + u, :], start=(u == 0), stop=(u == 1)),
                              reads=[r_cb, r_sq], writes=[r_ps], inc=(u == 1 and m % 4 == 3))
            rstd, r_rstd = rstd_r.next()
            for bnk in range(2):
                ps, r_ps = ss[bnk]
                act.op(lambda e: e.activation(out=rstd[:, bnk * 4:(bnk + 1) * 4, :].rearrange("p m t -> p (m t)"),
                                              in_=ps[:, :], func=AF.Ln, scale=1.0 / 256.0, bias=EPS),
                       reads=[r_ps], writes=[r_rstd])
            act.op(lambda e: e.activation(out=rstd[:], in_=rstd[:], func=AF.Exp, scale=-0.5),
                   reads=[r_rstd], writes=[r_rstd])
            dve.op(lambda e: e.tensor_tensor(out=on[:], in0=on[:], in1=rstd[:], op=ALU.mult),
                   reads=[r_on, r_rstd], writes=[r_on])
            aT, r_aTt = aT_r.next()
            pool.op(lambda e: e.tensor_tensor(out=aT[:], in0=on[:], in1=sg[:, :, sl], op=ALU.mult),
                    reads=[r_on, r_sg], writes=[r_aTt])
            pool.dma(aT_s[:, :, c * 128:(c + 1) * 128].rearrange("m p t -> p m t"), aT[:], reads=[r_aTt], writes=[r_aT])
    fw.barrier()
    sb.reset(base_mark)
    if dbg:
        d_aT = dout("d_aT", [8, 128, OWN], BF16)
        pool.dma(d_aT, aT_s, reads=[r_aT])
        d_Sf = dout("d_Sf", [NCH, 128, 1024], BF16)
        pool.dma(d_Sf, Sf_s, reads=r_Sf)
    if stop_after <= 1:
        return finish_zero()

    bT_s = nc.dram_tensor("bT_s", [8, 128, OWN], BF16).ap()
    r_bT = Res()
    lamt = sb.t([128, 256], F32)
    qkrow = sb.t([128, 128], F32)
    scal = sb.t([128, 16], F32)
    tmpq = sb.t([128, 128], F32)
    r_lamt, r_qkrow, r_scal, r_tmpq = Res(), Res(), Res(), Res()
    sp.dma(lamt[:], bass.AP(lam_d.tensor, 0, [[0, 128], [1, 256]]), writes=[r_lamt])
    sp.dma(qkrow[:], bass.AP(qkrow_d.tensor, 0, [[0, 128], [1, 128]]), writes=[r_qkrow])
    dve.op(lambda e: e.tensor_tensor(out=tmpq[:, 0:64], in0=lamt[:, 0:64], in1=lamt[:, 64:128], op=ALU.mult),
           reads=[r_lamt], writes=[r_tmpq])
    dve.op(lambda e: e.tensor_tensor(out=tmpq[:, 64:128], in0=lamt[:, 128:192], in1=lamt[:, 192:256], op=ALU.mult),
           reads=[r_lamt, r_tmpq], writes=[r_tmpq])
    dve.op(lambda e: e.reduce_sum(out=scal[:, 0:1], in_=tmpq[:, 0:64], axis=AX.X), reads=[r_tmpq], writes=[r_scal])
    dve.op(lambda e: e.reduce_sum(out=scal[:, 1:2], in_=tmpq[:, 64:128], axis=AX.X), reads=[r_tmpq, r_scal], writes=[r_scal])
    act.op(lambda e: e.activation(out=scal[:, 0:2], in_=scal[:, 0:2], func=AF.Exp), reads=[r_scal], writes=[r_scal])
    dve.op(lambda e: e.scalar_tensor_tensor(out=scal[:, 2:3], in0=scal[:, 1:2], scalar=-LAM_INIT, in1=scal[:, 0:1],
                                            op0=ALU.add, op1=ALU.subtract), reads=[r_scal], writes=[r_scal])
    dve.op(lambda e: e.tensor_scalar(out=tmpq[:], in0=qkrow[:], scalar1=-1.0, scalar2=None, op0=ALU.mult),
           reads=[r_qkrow, r_tmpq], writes=[r_tmpq])
    dve.op(lambda e: e.tensor_tensor(out=tmpq[:], in0=tmpq[:], in1=qkrow[:], op=ALU.max),
           reads=[r_qkrow, r_tmpq], writes=[r_tmpq])
    dve.op(lambda e: e.reduce_max(out=scal[:, 3:4], in_=tmpq[:, 0:64], axis=AX.X), reads=[r_tmpq, r_scal], writes=[r_scal])
    dve.op(lambda e: e.reduce_max(out=scal[:, 4:5], in_=tmpq[:, 64:128], axis=AX.X), reads=[r_tmpq, r_scal], writes=[r_scal])
    dve.op(lambda e: e.scalar_tensor_tensor(out=scal[:, 5:6], in0=scal[:, 3:4], scalar=-8.0, in1=scal[:, 4:5],
                                            op0=ALU.mult, op1=ALU.mult), reads=[r_scal], writes=[r_scal])
    dve.op(lambda e: e.tensor_scalar(out=scal[:, 6:7], in0=smallv[:, 4:5], scalar1=(1.0 - LAM_INIT), scalar2=None,
                                     op0=ALU.mult), reads=[r_small, r_scal], writes=[r_scal])
    dve.op(lambda e: e.tensor_scalar(out=scal[:, 7:8], in0=smallv[:, 2:3], scalar1=0.125, scalar2=None,
                                     op0=ALU.mult), reads=[r_small, r_scal], writes=[r_scal])
    ones_bf = sb.t([128, 32], BF16)
    r_onesbf = Res()
    dve.op(lambda e: e.memset(ones_bf[:], 1.0), writes=[r_onesbf])
    sel = sb.t([64, 2, 128], BF16)
    r_sel = Res()
    dve.op(lambda e: e.memset(sel[:], 0.0), writes=[r_sel])
    dve.op(lambda e: e.memset(sel[0:1, 0, :], 1.0), writes=[r_sel])
    dve.op(lambda e: e.memset(sel[32:33, 1, :], 1.0), writes=[r_sel])

    if cut == 1:
        fw.barrier()
        return finish_zero()
    wst_r = Ring([sb.t([128, 8, 128], F32) for _ in range(3)])
    wh_r = Ring([sb.t([128, 8, 512], BF16) for _ in range(2)])
    KT = sb.t([128, NTOK], BF16)
    Vt = sb.t([128, NTOK // 128, 128], BF16)
    QT = sb.t([128, OWN], BF16)
    sgT = sb.t([128, OWN], BF16)
    gpre = sb.t([128, OWN], F32)
    r_KT, r_V, r_QT, r_sgT, r_gpre = Res(), Res(), Res(), Res(), Res()
    hg_r = Ring([sb.t([128, 8, 512], BF16) for _ in range(2)])
    cos_r = Ring([sb.t([128, 512], F32) for _ in range(2)])
    sin_r = Ring([sb.t([128, 512], F32) for _ in range(2)])
    sqb_r = Ring([sb.t([128, 512], BF16) for _ in range(2)])
    kgb_r = Ring([sb.t([128, 512], BF16) for _ in range(2)])
    f1_r = Ring([sb.t([128, 512], F32) for _ in range(2)])
    f2_r = Ring([sb.t([128, 512], F32) for _ in range(2)])
    f3_r = Ring([sb.t([128, 512], F32) for _ in range(2)])
    P_r = Ring([sb.t([128, 1024], BF16) for _ in range(3)])
    rl_r = Ring([sb.t([64, 512], F32) for _ in range(1)])
    rh_r = Ring([sb.t([64, 2, 512], BF16) for _ in range(1)])
    o1_r = Ring([sb.t([128, 512], F32) for _ in range(1)])
    o2_r = Ring([sb.t([128, 512], F32) for _ in range(1)])
    bo_r = Ring([sb.t([128, 512], BF16) for _ in range(2)])
    bres = [Res(True) for _ in range(8)]
    s_ring = Ring([pairs[0], pairs[1]], res=[(bres[0], bres[1]), (bres[2], bres[3])])
    O1, O2, Lb, Mb = banks[4], banks[5], banks[6], banks[7]
    r_O1, r_O2, r_L, r_M = bres[4], bres[5], bres[6], bres[7]
    bring = Ring(banks[0:4], res=bres[0:4])
    mring = Ring(banks[4:8], res=bres[4:8])

    def rope_norm(ps, r_ps, n, gain_ap, r_gain, cosg, r_cos, sing, r_sin, out_ap, r_out):
        sq, r_sq = sqb_r.next()
        kg, r_kg = kgb_r.next()
        act.op(lambda e: e.activation(out=sq[:, 0:n], in_=ps[:, 0:n], func=AF.Square), reads=[r_ps], writes=[r_sq])
        act.op(lambda e: e.activation(out=kg[:, 0:n], in_=ps[:, 0:n], func=AF.Copy, scale=gain_ap),
               reads=[r_ps, r_gain], writes=[r_kg])
        p2, r_p2 = mring.next()
        p3, r_p3 = mring.next()
        pe.op(lambda e: e.matmul(p2[:, 0:n], lhsT=cb[:, C_BLK64, :], rhs=sq[:, 0:n], start=True, stop=True),
              reads=[r_cb, r_sq], writes=[r_p2])
        pe.op(lambda e: e.matmul(p3[:, 0:n], lhsT=cb[:, C_PERM, :], rhs=kg[:, 0:n], start=True, stop=True),
              reads=[r_cb, r_kg], writes=[r_p3])
        f3, r_f3 = f3_r.next()
        act.op(lambda e: e.activation(out=f3[:, 0:n], in_=p2[:, 0:n], func=AF.Ln, scale=1.0 / 64.0, bias=EPS),
               reads=[r_p2], writes=[r_f3])
        act.op(lambda e: e.activation(out=f3[:, 0:n], in_=f3[:, 0:n], func=AF.Exp, scale=-0.5),
               reads=[r_f3], writes=[r_f3])
        f1, r_f1 = f1_r.next()
        f2, r_f2 = f2_r.next()
        dve.op(lambda e: e.scalar_tensor_tensor(out=f1[:, 0:n], in0=ps[:, 0:n], scalar=gain_ap,
                                                in1=cosg[:, 0:n], op0=ALU.mult, op1=ALU.mult),
               reads=[r_ps, r_gain, r_cos], writes=[r_f1])
        dve.op(lambda e: e.tensor_tensor(out=f2[:, 0:n], in0=p3[:, 0:n], in1=sing[:, 0:n], op=ALU.mult),
               reads=[r_p3, r_sin], writes=[r_f2])
        dve.op(lambda e: e.tensor_tensor(out=f1[:, 0:n], in0=f1[:, 0:n], in1=f2[:, 0:n], op=ALU.add),
               reads=[r_f1, r_f2], writes=[r_f1])
        pool.op(lambda e: e.tensor_tensor(out=out_ap, in0=f1[:, 0:n], in1=f3[:, 0:n], op=ALU.mult),
                reads=[r_f1, r_f3], writes=[r_out])

    groups2 = [(0, 256)] + [(CTX + g * 512, 512) for g in range(SEQ // 512)]
    sqp = sb.t([128, 512], BF16)
    r_sqp = Res()
    carry = None
    for h in range(NHEADS_RUN):
        wh, r_wh = wh_r.next()
        for si, off in enumerate((O_DQ, O_DK, O_DV, O_DG)):
            st, r_st = wst_r.next()
            sp.dma(st[:], win_d[:, off + h * 128:off + (h + 1) * 128].rearrange("(j p) c -> p j c", p=128), writes=[r_st])
            (dve if si % 2 == 0 else pool).op(
                lambda e: e.tensor_copy(out=wh[:, :, si * 128:(si + 1) * 128], in_=st[:]), reads=[r_st], writes=[r_wh])
        if cut == 2:
            fw.barrier()
            return finish_zero()
        for gi, (tok0, n) in enumerate(groups2):
            if cut == 3 and gi == 1:
                fw.barrier()
                return finish_zero()
            if cut == 4 and gi == 2:
                fw.barrier()
                return finish_zero()
            hg, r_hg = hg_r.next()
            sp.dma(hg[:, :, 0:n], hT_s[:, :, tok0:tok0 + n].rearrange("j p t -> p j t"), reads=[r_hT], writes=[r_hg])
            cosg, r_cos = cos_r.next()
            sing, r_sin = sin_r.next()
            sp.dma(cosg[:, 0:n], cos_d[:, tok0:tok0 + n], writes=[r_cos])
            sp.dma(sing[:, 0:n], sin_d[:, tok0:tok0 + n], writes=[r_sin])
            own = 1 <= gi <= 8
            nt = n // 128
            psK, r_psK = bring.next()
            for j in range(8):
                pe.op(lambda e: e.matmul(psK[:, 0:n], lhsT=wh[:, j, 128:256], rhs=hg[:, j, 0:n], start=(j == 0), stop=(j == 7)),
                      reads=[r_wh, r_hg], writes=[r_psK], inc=(j == 7))
            psV, r_psV = bring.next()
            for i in range(nt):
                for j in range(8):
                    pe.op(lambda e: e.matmul(psV[:, i * 128:(i + 1) * 128], lhsT=hg[:, j, i * 128:(i + 1) * 128],
                                             rhs=wh[:, j, 256:384], start=(j == 0), stop=(j == 7)),
                          reads=[r_wh, r_hg], writes=[r_psV], inc=(j == 7 and i == nt - 1))
            if own:
                osl = slice((gi - 1) * 512, gi * 512)
                psQ, r_psQ = bring.next()
                for j in range(8):
                    pe.op(lambda e: e.matmul(psQ[:, :], lhsT=wh[:, j, 0:128], rhs=hg[:, j, :], start=(j == 0), stop=(j == 7)),
                          reads=[r_wh, r_hg], writes=[r_psQ], inc=(j == 7))
                psG, r_psG = bring.next()
                for j in range(8):
                    pe.op(lambda e: e.matmul(psG[:, :], lhsT=wh[:, j, 384:512], rhs=hg[:, j, :], start=(j == 0), stop=(j == 7)),
                          reads=[r_wh, r_hg], writes=[r_psG], inc=(j == 7))
            t0 = tok0 // 128
            dve.op(lambda e: e.tensor_copy(out=Vt[:, t0:t0 + nt, :].rearrange("p a b -> p (a b)"), in_=psV[:, 0:n]),
                   reads=[r_psV], writes=[r_V])
            if carry is not None and 1 <= gi <= 5:
                for stg in {1: (0,), 2: (1,), 3: (2,), 4: (3, 4), 5: (5,)}[gi]:
                    carry[0](carry[1], stg)
                if gi == 5:
                    carry = None
            rope_norm(psK, r_psK, n, smallv[:, 3:4], r_small, cosg, r_cos, sing, r_sin, KT[:, tok0:tok0 + n], r_KT)
            if own:
                rope_norm(psQ, r_psQ, 512, scal[:, 7:8], r_scal, cosg, r_cos, sing, r_sin, QT[:, osl], r_QT)
                act.op(lambda e: e.activation(out=gpre[:, osl], in_=psG[:, :], func=AF.Copy), reads=[r_psG], writes=[r_gpre])
        act.op(lambda e: e.activation(out=gpre[:], in_=gpre[:], func=AF.Silu), reads=[r_gpre], writes=[r_gpre])
        dve.op(lambda e: e.tensor_scalar(out=sgT[:], in0=gpre[:], scalar1=scal[:, 6:7], scalar2=None, op0=ALU.mult),
               reads=[r_gpre, r_scal], writes=[r_sgT])
        NKT = NTOK // 128
        steps = [(qg, kt) for qg in range(NQG_RUN) for kt in range(NKT)]
        sbuf_S = {}

        def emit_S(i):
            qg, kt = steps[i]
            qsl = slice(qg * 512, (qg + 1) * 512)
            ksl = slice(kt * 128, (kt + 1) * 128)
            sp_, r_S = s_ring.next()
            pe.op(lambda e: e.matmul(sp_[:, 0, :], lhsT=KT[0:64, ksl], rhs=QT[0:64, qsl], start=True, stop=True),
                  reads=[r_KT, r_QT], writes=[*r_S], inc=False)
            pe.op(lambda e: e.matmul(sp_[:, 1, :], lhsT=KT[64:128, ksl], rhs=QT[64:128, qsl], start=True, stop=True,
                                     tile_position=(64, 0)),
                  reads=[r_KT, r_QT], writes=[*r_S])
            sbuf_S[i] = (sp_, r_S)

        sbuf_P = {}

        def emit_exp(i):
            sp_, r_S = sbuf_S.pop(i)
            P, r_P = P_r.next()
            act.op(lambda e: e.activation(out=P[:], in_=sp_[:].rearrange("p a b -> p (a b)"), func=AF.Exp,
                                          bias=scal[:, 5:6]),
                   reads=[*r_S, r_scal], writes=[r_P])
            sbuf_P[i] = (P, r_P)

        def emit_AV(i):
            qg, kt = steps[i]
            P, r_P = sbuf_P.pop(i)
            first, last = (kt == 0), (kt == NKT - 1)
            pe.op(lambda e: e.matmul(O1[:, :], lhsT=Vt[:, kt, :], rhs=P[:, 0:512], start=first, stop=last),
                  reads=[r_V, r_P], writes=[r_O1], inc=False)
            pe.op(lambda e: e.matmul(O2[:, :], lhsT=Vt[:, kt, :], rhs=P[:, 512:1024], start=first, stop=last),
                  reads=[r_V, r_P], writes=[r_O2], inc=False)
            pe.op(lambda e: e.matmul(Lb[0:32, :], lhsT=ones_bf[:, :], rhs=P[:, 0:512], start=first, stop=last,
                                     tile_position=(0, 0)),
                  reads=[r_onesbf, r_P], writes=[r_L], inc=False)
            pe.op(lambda e: e.matmul(Lb[32:64, :], lhsT=ones_bf[:, :], rhs=P[:, 512:1024], start=first, stop=last,
                                     tile_position=(0, 32)),
                  reads=[r_onesbf, r_P], writes=[r_L])

        def post_a(qg):
            rl, r_rl = rl_r.next()
            o1, r_o1 = o1_r.next()
            o2, r_o2 = o2_r.next()
            rh, r_rh = rh_r.next()
            dve.op(lambda e: e.tensor_copy(out=rl[:], in_=Lb[0:64, :]), reads=[r_L], writes=[r_rl])
            dve.op(lambda e: e.tensor_copy(out=o1[:], in_=O1[:, :]), reads=[r_O1], writes=[r_o1])
            dve.op(lambda e: e.tensor_copy(out=o2[:], in_=O2[:, :]), reads=[r_O2], writes=[r_o2])
            dve.op(lambda e: e.reciprocal(out=rl[:], in_=rl[:]), reads=[r_rl], writes=[r_rl])
            dve.op(lambda e: e.tensor_scalar(out=rl[32:64, :], in0=rl[32:64, :], scalar1=scal[32:64, 2:3], scalar2=None,
                                             op0=ALU.mult), reads=[r_rl, r_scal], writes=[r_rl])
            dve.op(lambda e: e.tensor_copy(out=rh[:, 0, :], in_=rl[:]), reads=[r_rl], writes=[r_rh])
            dve.op(lambda e: e.tensor_tensor(out=rh[:, 1, :], in0=rl[:], in1=rh[:, 0, :], op=ALU.subtract),
                   reads=[r_rl, r_rh], writes=[r_rh])
            return dict(qg=qg, rl=(rl, r_rl), o1=(o1, r_o1), o2=(o2, r_o2), rh=(rh, r_rh))

        def post_b(st, stage, hh=h):
            o1, r_o1 = st["o1"]
            o2, r_o2 = st["o2"]
            rh, r_rh = st["rh"]
            qsl = slice(st["qg"] * 512, (st["qg"] + 1) * 512)
            if stage in (0, 1):
                osb, r_osb = (o1, r_o1) if stage == 0 else (o2, r_o2)
                for part in range(2):
                    pe.op(lambda e: e.matmul(Mb[:, :], lhsT=sel[:, stage, :], rhs=rh[:, part, :],
                                             start=(part == 0), stop=(part == 1)),
                          reads=[r_sel, r_rh], writes=[r_M], inc=(part == 1))
                dve.op(lambda e: e.tensor_tensor(out=osb[:], in0=osb[:], in1=Mb[:, :], op=ALU.mult),
                       reads=[r_osb, r_M], writes=[r_osb])
            elif stage == 2:
                dve.op(lambda e: e.tensor_tensor(out=o1[:], in0=o1[:], in1=o2[:], op=ALU.add),
                       reads=[r_o1, r_o2], writes=[r_o1])
                sq, r_sq = sqp, r_sqp
                st["sq"] = (sq, r_sq)
                dve.op(lambda e: e.tensor_tensor(out=sq[:], in0=o1[:], in1=o1[:], op=ALU.mult), reads=[r_o1], writes=[r_sq])
            elif stage == 3:
                sq, r_sq = st["sq"]
                pe.op(lambda e: e.matmul(Mb[:, :], lhsT=cb[:, C_ONES, :], rhs=sq[:], start=True, stop=True),
                      reads=[r_cb, r_sq], writes=[r_M])
            elif stage == 4:
                act.op(lambda e: e.activation(out=o2[:], in_=Mb[:, :], func=AF.Ln, scale=1.0 / 128.0, bias=EPS),
                       reads=[r_M, r_o2], writes=[r_o2])
                act.op(lambda e: e.activation(out=o2[:], in_=o2[:], func=AF.Exp, scale=-0.5), reads=[r_o2], writes=[r_o2])
            elif stage == 5:
                dve.op(lambda e: e.tensor_tensor(out=o1[:], in0=o1[:], in1=o2[:], op=ALU.mult),
                       reads=[r_o1, r_o2], writes=[r_o1])
                bo, r_bo = bo_r.next()
                dve.op(lambda e: e.tensor_tensor(out=bo[:], in0=o1[:], in1=sgT[:, qsl], op=ALU.mult),
                       reads=[r_o1, r_sgT], writes=[r_bo])
                pool.dma(bT_s[hh][:, qsl], bo[:], reads=[r_bo], writes=[r_bT])

        POST_AT = {8: 0, 11: 1, 14: 2, 17: 3, 20: 4, 23: 5}
        for i0 in range(min(2, len(steps))):
            emit_S(i0)
        pending = None
        for i, (qg, kt) in enumerate(steps):
            emit_exp(i)
            if i + 2 < len(steps):
                emit_S(i + 2)
            emit_AV(i)
            if pending is not None and kt in POST_AT:
                post_b(pending, POST_AT[kt])
                if POST_AT[kt] == 5:
                    pending = None
            if kt == NKT - 1:
                pending = post_a(qg)
                if qg == NQG_RUN - 1:
                    if h == NHEADS_RUN - 1:
                        for stg in range(6):
                            post_b(pending, stg)
                    else:
                        carry = (post_b, pending)
                    pending = None
    fw.barrier()
    sb.reset(base_mark)
    if dbg:
        d_bT = dout("d_bT", [8, 128, OWN], BF16)
        if NQG_RUN > 0:
            pool.dma(d_bT[0:NHEADS_RUN, :, 0:NQG_RUN * 512], bT_s[0:NHEADS_RUN, :, 0:NQG_RUN * 512], reads=[r_bT])
    if stop_after <= 2:
        return finish_zero()

    wm = sb.t([128, 8, 2048], BF16)
    wbg = sb.t([128, 8, D], BF16)
    wbd = sb.t([128, 8, D], BF16)
    wo = sb.t([128, 8, D], BF16)
    r_wm, r_wbg, r_wbd, r_wo = Res(), Res(), Res(), Res()
    wst_r = Ring([sb.t([128, 8, 256], F32) for _ in range(2)])
    loads = [(wm, r_wm, win_d, O_MG + c * 256, c * 256) for c in range(8)]
    for (wt, r_wt, src) in ((wbg, r_wbg, wbg_d), (wbd, r_wbd, wbd_d), (wo, r_wo, wo_d)):
        loads += [(wt, r_wt, src, c * 256, c * 256) for c in range(4)]
    def emit_loads(k0, k1):
        for (wt, r_wt, src, c0, d0) in loads[k0:k1]:
            st, r_st = wst_r.next()
            sp.dma(st[:], src[:, c0:c0 + 256].rearrange("(j p) c -> p j c", p=128), writes=[r_st])
            pool.op(lambda e: e.tensor_copy(out=wt[:, :, d0:d0 + 256], in_=st[:]), reads=[r_st], writes=[r_wt])

    emit_loads(0, 8)
    hg_r = Ring([sb.t([128, 8, 512], BF16) for _ in range(2)])
    ag_r = Ring([sb.t([128, 8, 512], BF16) for _ in range(2)])
    bg_r = Ring([sb.t([128, 8, 512], BF16) for _ in range(2)])
    sig_r = Ring([sb.t([128, 16, 512], BF16) for _ in range(1)])
    yT_r = Ring([sb.t([128, 8, 512], BF16) for _ in range(1)])
    t1_r = Ring([sb.t([128, 512], F32) for _ in range(2)])
    t2_r = Ring([sb.t([128, 512], F32) for _ in range(2)])
    xt_r = Ring([sb.t([128, D], F32) for _ in range(3)])
    ot_r = Ring([sb.t([128, D], F32) for _ in range(2)])
    pring = Ring(banks, psum=True)
    r_out = Res()
    for g in range(OWN // 512):
        tsl = slice(g * 512, (g + 1) * 512)
        hg, r_hg = hg_r.next()
        ag, r_ag = ag_r.next()
        bg, r_bg = bg_r.next()
        sp.dma(hg[:], hT_s[:, :, CTX + g * 512:CTX + (g + 1) * 512].rearrange("j p t -> p j t"), reads=[r_hT], writes=[r_hg])
        sp.dma(ag[:], aT_s[:, :, tsl].rearrange("j p t -> p j t"), reads=[r_aT], writes=[r_ag])
        sp.dma(bg[:], bT_s[:, :, tsl].rearrange("j p t -> p j t"), reads=[r_bT], writes=[r_bg])
        sig, r_sig = sig_r.next()
        for m in range(16):
            ps, r_ps = pring.next()
            for j in range(8):
                pe.op(lambda e: e.matmul(ps[:, :], lhsT=wm[:, j, m * 128:(m + 1) * 128], rhs=hg[:, j, :],
                                         start=(j == 0), stop=(j == 7)), reads=[r_wm, r_hg], writes=[r_ps], inc=(j == 7))
            act.op(lambda e: e.activation(out=sig[:, m, :], in_=ps[:, :], func=AF.Sigmoid), reads=[r_ps], writes=[r_sig])
        if g == 0:
            emit_loads(8, 20)
        yT, r_yT = yT_r.next()
        for m in range(8):
            psA, r_psA = pring.next()
            psB, r_psB = pring.next()
            for j in range(8):
                pe.op(lambda e: e.matmul(psA[:, :], lhsT=wbg[:, j, m * 128:(m + 1) * 128], rhs=ag[:, j, :],
                                         start=(j == 0), stop=(j == 7)), reads=[r_wbg, r_ag], writes=[r_psA], inc=(j == 7))
            for j in range(8):
                pe.op(lambda e: e.matmul(psB[:, :], lhsT=wbd[:, j, m * 128:(m + 1) * 128], rhs=bg[:, j, :],
                                         start=(j == 0), stop=(j == 7)), reads=[r_wbd, r_bg], writes=[r_psB], inc=(j == 7))
            t1, r_t1 = t1_r.next()
            t2, r_t2 = t2_r.next()
            dve.op(lambda e: e.tensor_tensor(out=t1[:], in0=psA[:, :], in1=sig[:, m, :], op=ALU.mult),
                   reads=[r_psA, r_sig], writes=[r_t1])
            dve.op(lambda e: e.tensor_tensor(out=t2[:], in0=psB[:, :], in1=sig[:, 8 + m, :], op=ALU.mult),
                   reads=[r_psB, r_sig], writes=[r_t2])
            pool.op(lambda e: e.tensor_tensor(out=yT[:, m, :], in0=t1[:], in1=t2[:], op=ALU.add),
                    reads=[r_t1, r_t2], writes=[r_yT])
        for i in range(4):
            xt, r_xt = xt_r.next()
            row0 = g * 512 + i * 128
            sp.dma(xt[:], x_d[row0:row0 + 128, :], writes=[r_xt])
            ot, r_ot = ot_r.next()
            for cblk in range(2):
                csl = slice(cblk * 512, (cblk + 1) * 512)
                ps, r_ps = pring.next()
                for m in range(8):
                    pe.op(lambda e: e.matmul(ps[:, :], lhsT=yT[:, m, i * 128:(i + 1) * 128], rhs=wo[:, m, csl],
                                             start=(m == 0), stop=(m == 7)), reads=[r_wo, r_yT], writes=[r_ps], inc=(m == 7))
                t1, r_t1 = t1_r.next()
                dve.op(lambda e: e.tensor_tensor(out=t1[:], in0=ps[:, :], in1=gxb[:, csl], op=ALU.mult),
                       reads=[r_ps, r_gxb], writes=[r_t1])
                pool.op(lambda e: e.tensor_tensor(out=ot[:, csl], in0=t1[:], in1=xt[:, csl], op=ALU.add),
                        reads=[r_t1, r_xt], writes=[r_ot])
            pool.dma(out_d[row0:row0 + 128, :], ot[:], reads=[r_ot], writes=[r_out])
    fw.barrier()
    return nc, dbg_out


def _const_mats():
    m = np.zeros((NCB, 128, 128), np.float32)
    m[0] = np.eye(128)
    for mm in range(128):
        k = mm + 16 if (mm % 32) < 16 else mm - 16
        m[1][k, mm] = 1.0
    for k in range(128):
        for mm in range(128):
            if k // 64 == mm // 64:
                m[2][k, mm] = 1.0
    m[3][:] = 1.0
    j = np.arange(128)[:, None]
    i = np.arange(128)[None, :]
    sc = -1.0 / 16.0
    m[4] = sc * (j > i)
    m[5] = sc * (j < i)
    m[6] = sc * (j <= i)
    m[7] = sc * (j >= i)
    m[8] = sc * ((j <= 63).astype(np.float32) - (j <= i))
    m[9] = sc * ((j >= 64).astype(np.float32) - (j >= i))
    m[10] = (j <= i)
    m[11] = (j >= i)
    return m


def _rope_tables(positions):
    inv = 10000.0 ** (-np.arange(0, 32, 2, dtype=np.float64) / 32.0)
    row = (positions // 64).astype(np.float64)
    col = (positions % 64).astype(np.float64)
    cos = np.zeros((128, len(positions)), np.float64)
    sin = np.zeros((128, len(positions)), np.float64)
    for p in range(128):
        f = p % 64
        blk = f // 32
        i = f % 32
        ang = (row if blk == 0 else col) * inv[i % 16]
        cos[p] = np.cos(ang)
        sin[p] = (-1.0 if i < 16 else 1.0) * np.sin(ang)
    return cos, sin


def make_in_maps(x, c, ctx, c_ctx, w_ada, b_ada, w_in, gla_w_decay, gla_b_decay, gla_norm,
                 diff_q_norm, diff_k_norm, diff_lambda, diff_norm, w_br_gla, w_br_diff, w_out):
    f32 = np.float32
    x = np.asarray(x, f32)
    ctx = np.asarray(ctx, f32)
    c = np.asarray(c, f32)
    c_ctx = np.asarray(c_ctx, f32)
    w_ada0 = np.ascontiguousarray(np.asarray(w_ada, f32)[0])
    b_ada0 = np.asarray(b_ada, f32)[0]
    w_in0 = np.asarray(w_in, f32)[0]
    w_in_rev = w_in0.copy()
    w_in_rev[:, O_LR:O_LR + 16] = w_in0[:, O_LR + 16:O_LR + 32]
    w_in_rev[:, O_LR + 16:O_LR + 32] = w_in0[:, O_LR:O_LR + 16]
    wd = np.asarray(gla_w_decay, f32)[0]
    bd = np.asarray(gla_b_decay, f32)[0]
    wdec = np.zeros((2, 17, 512), f32)
    wdec[:, :16] = wd
    wdec[:, 16] = bd
    wdec_rev = np.ascontiguousarray(wdec[::-1])
    cbf = _const_mats().transpose(1, 0, 2).astype(NPBF)
    cbf = np.ascontiguousarray(cbf)
    common = {
        "w_ada": w_ada0,
        "b_ada": np.ascontiguousarray(b_ada0.reshape(24, 128).T),
        "b_ada_g": np.ascontiguousarray(b_ada0[2048:3072].reshape(1, D)),
        "gla_norm": np.ascontiguousarray(np.asarray(gla_norm, f32)[0].reshape(2, 128).T),
        "q_norm": np.ascontiguousarray(np.tile(np.asarray(diff_q_norm, f32)[0], 2).reshape(128, 1)),
        "k_norm": np.ascontiguousarray(np.tile(np.asarray(diff_k_norm, f32)[0], 2).reshape(128, 1)),
        "diff_norm": np.ascontiguousarray(np.asarray(diff_norm, f32)[0].reshape(128, 1)),
        "lam": np.ascontiguousarray(np.asarray(diff_lambda, f32)[0].reshape(1, 256)),
        "qk_row": np.ascontiguousarray(np.concatenate([np.asarray(diff_q_norm, f32)[0], np.asarray(diff_k_norm, f32)[0]]).reshape(1, 128)),
        "w_br_gla": np.ascontiguousarray(np.asarray(w_br_gla, f32)[0]),
        "w_br_diff": np.ascontiguousarray(np.asarray(w_br_diff, f32)[0]),
        "w_out": np.ascontiguousarray(np.asarray(w_out, f32)[0]),
        "cbf": cbf,
    }
    pos = np.arange(SEQ)
    tabs = []
    for rev in (False, True):
        p = pos[::-1] if rev else pos
        cs, sn = _rope_tables(p)
        cos = np.ones((128, NTOK), f32)
        sin = np.zeros((128, NTOK), f32)
        cos[:, CTX:] = cs
        sin[:, CTX:] = sn
        tabs.append((cos, sin))
    maps = []
    for core in range(8):
        b, half = core // 2, core % 2
        rev = half == 1
        xs = x[b][::-1] if rev else x[b]
        cx = ctx[b][::-1] if rev else ctx[b]
        cvec = np.stack([c[b].reshape(8, 128).T, c_ctx.reshape(8, 128).T], axis=-1)
        m = dict(common)
        m["x"] = np.ascontiguousarray(xs)
        m["ctx"] = np.ascontiguousarray(cx)
        m["cvec"] = np.ascontiguousarray(cvec.astype(f32))
        m["w_in"] = w_in_rev if rev else w_in0
        m["w_dec"] = wdec_rev if rev else wdec
        m["cos_t"], m["sin_t"] = tabs[1 if rev else 0]
        maps.append(m)
    return maps


def assemble(results):
    out = np.zeros((4, SEQ, D), np.float32)
    for core in range(8):
        b, half = core // 2, core % 2
        o = results[core]["out"]
        if half == 0:
            out[b, :OWN] = o
        else:
            out[b, OWN:] = o[::-1]
    return out


_NC_CACHE = {}


def kernel(**inputs):
    if "nc" not in _NC_CACHE:
        _NC_CACHE["nc"] = build_program()[0]
    nc = _NC_CACHE["nc"]
    maps = make_in_maps(**inputs)
    res = run_bass_kernel_spmd(nc, maps, core_ids=list(range(8)))
    return assemble(res.results)
```

```python
import math
import numpy as np
import ml_dtypes
import concourse.bass as bass
import concourse.mybir as mybir
from concourse.bass_utils import run_bass_kernel_spmd

F32 = mybir.dt.float32
BF16 = mybir.dt.bfloat16
AF = mybir.ActivationFunctionType
ALU = mybir.AluOpType
AX = mybir.AxisListType
NPBF = ml_dtypes.bfloat16

D = 1024
SEQ = 8192
OWN = 4096
CTX = 256
NTOK = CTX + SEQ
EPS = 1e-6
IN_W = 9248
O_GQ, O_GK, O_GV, O_GG, O_LR, O_DQ, O_DK, O_DV, O_DG, O_MG, O_MD = (
    0, 512, 1024, 2048, 3072, 3104, 4128, 5152, 6176, 7200, 8224)
LAM_INIT = 0.8 - 0.6 * math.exp(-0.3 * 0)
SB_LO = 16640
SB_HI = 229344
NCB = 12
NCH = OWN // 128
LN_QS = -0.5 * math.log(128.0)


class Res:
    __slots__ = ("w", "r", "psum")

    def __init__(self, psum=False):
        self.w = None
        self.r = []
        self.psum = psum


class Eng:
    def __init__(self, nc, name, h, is_pe=False):
        self.nc = nc
        self.name = name
        self.h = h
        self.sem = nc.alloc_semaphore("s_" + name)
        self.count = 0
        self.seen = {}
        self.is_pe = is_pe
        self.dpool = None
        self.dn = 0

    def _wait(self, tok):
        sem, val = tok
        if sem is self.sem and self.is_pe:
            return
        key = id(sem)
        if self.seen.get(key, 0) >= val:
            return
        self.h.wait_ge(sem, val)
        self.seen[key] = val

    def _deps(self, reads, writes):
        for r in reads:
            if r.w is not None:
                self._wait(r.w)
            if r.psum:
                for t in r.r:
                    if t[0] is not self.sem:
                        self._wait(t)
        for w in writes:
            if w.w is not None:
                self._wait(w.w)
            for t in w.r:
                self._wait(t)

    def _mark(self, tok, reads, writes):
        for r in reads:
            if len(r.r) > 24:
                d = {}
                for s, v in r.r:
                    if d.get(id(s), (None, 0))[1] < v:
                        d[id(s)] = (s, v)
                r.r = list(d.values())
            r.r.append(tok)
        for w in writes:
            w.w = tok
            w.r = []

    def op(self, fn, reads=(), writes=(), inc=True):
        self._deps(reads, writes)
        ins = fn(self.h)
        if inc:
            self.count += 1
            ins.then_inc(self.sem, 1)
            tok = (self.sem, self.count)
        else:
            tok = (self.sem, self.count + 1)
        self._mark(tok, reads, writes)
        return ins

    def init_dma(self, npool):
        self.dpool = [[self.nc.alloc_semaphore("d_%s_%d" % (self.name, i)), 0] for i in range(npool)]

    def dma(self, out, in_, reads=(), writes=()):
        slot = self.dpool[self.dn % len(self.dpool)]
        self.dn += 1
        if slot[1] > 0:
            self._wait((slot[0], slot[1]))
        self._deps(reads, writes)
        slot[1] += 16
        ins = self.h.dma_start(out=out, in_=in_)
        ins.then_inc(slot[0], 16)
        self._mark((slot[0], slot[1]), reads, writes)
        return ins


class FW:
    def __init__(self, nc):
        self.nc = nc
        self.pe = Eng(nc, "pe", nc.tensor, is_pe=True)
        self.act = Eng(nc, "act", nc.scalar)
        self.dve = Eng(nc, "dve", nc.vector)
        self.pool = Eng(nc, "pool", nc.gpsimd)
        self.sp = Eng(nc, "sp", nc.sync)
        self.sp.init_dma(24)
        self.pool.init_dma(12)
        self.engs = [self.pe, self.act, self.dve, self.pool, self.sp]

    def barrier(self):
        toks = []
        for e in self.engs:
            if e.count > 0:
                toks.append((e.sem, e.count))
            if e.dpool:
                for s, c in e.dpool:
                    if c > 0:
                        toks.append((s, c))
        for e in self.engs:
            for t in toks:
                e._wait(t)


class SB:
    def __init__(self, nc):
        self.nc = nc
        self.off = SB_LO
        self.n = 0

    def mark(self):
        return self.off

    def reset(self, m):
        self.off = m

    def t(self, shape, dt):
        nb = int(np.prod(shape[1:])) * (4 if dt == F32 else 2)
        nb = (nb + 63) // 64 * 64
        assert self.off + nb <= SB_HI, ("SBUF overflow", self.off, nb)
        self.n += 1
        h = self.nc.alloc_sbuf_tensor_at("sb%d" % self.n, list(shape), dt, offset=self.off)
        self.off += nb
        return h


class Ring:
    def __init__(self, items, res=None, psum=False):
        if res is None:
            res = [Res(psum) for _ in items]
        self.items = list(zip(items, res))
        self.i = 0

    def next(self):
        it = self.items[self.i % len(self.items)]
        self.i += 1
        return it


def build_program(stop_after=99, dbg=False, NHEADS_RUN=8, NQG_RUN=8, cut=0):
    nc = bass.Bass("TRN2", target_bir_lowering=False)
    fw = FW(nc)
    pe, act, dve, pool, sp = fw.pe, fw.act, fw.dve, fw.pool, fw.sp
    sb = SB(nc)

    def din(name, shape, dt=F32):
        return nc.dram_tensor(name, list(shape), dt, kind="ExternalInput").ap()

    x_d = din("x", [SEQ, D])
    ctx_d = din("ctx", [CTX, D])
    cvec_d = din("cvec", [128, 8, 2])
    wada_d = din("w_ada", [D, 3 * D])
    bada_d = din("b_ada", [128, 24])
    badag_d = din("b_ada_g", [1, D])
    win_d = din("w_in", [D, IN_W])
    wdec_d = din("w_dec", [2, 17, 512])
    glan_d = din("gla_norm", [128, 2])
    qn_d = din("q_norm", [128, 1])
    kn_d = din("k_norm", [128, 1])
    dn_d = din("diff_norm", [128, 1])
    lam_d = din("lam", [1, 256])
    qkrow_d = din("qk_row", [1, 128])
    wbg_d = din("w_br_gla", [D, D])
    wbd_d = din("w_br_diff", [D, D])
    wo_d = din("w_out", [D, D])
    cos_d = din("cos_t", [128, NTOK])
    sin_d = din("sin_t", [128, NTOK])
    cb_d = din("cbf", [128, NCB, 128], BF16)
    out_d = nc.dram_tensor("out", [OWN, D], F32, kind="ExternalOutput").ap()

    hT_s = nc.dram_tensor("hT_s", [8, 128, NTOK], BF16).ap()
    r_hT = Res()
    dbg_out = {}

    def dout(name, shape, dt=F32):
        t = nc.dram_tensor(name, list(shape), dt, kind="ExternalOutput").ap()
        dbg_out[name] = t
        return t

    pairs = [nc.alloc_psum_tensor("pp%d" % i, [128, 2, 512], F32) for i in range(4)]
    banks = [pairs[i // 2][:, i % 2, :] for i in range(8)]

    cb = sb.t([128, NCB, 128], BF16)
    r_cb = Res()
    sp.dma(cb[:], cb_d, writes=[r_cb])
    C_ID, C_PERM, C_BLK64, C_ONES, C_M1F, C_M1B, C_TRIF, C_TRIB, C_RDF, C_RDB, C_MSKF, C_MSKB = range(12)
    modT = sb.t([128, 24, 2], F32)
    r_mod = Res()
    gxb = sb.t([128, D], F32)
    r_gxb = Res()
    smallv = sb.t([128, 8], F32)
    r_small = Res()
    sp.dma(smallv[:, 0:2], glan_d, writes=[r_small])
    sp.dma(smallv[:, 2:3], qn_d, writes=[r_small])
    sp.dma(smallv[:, 3:4], kn_d, writes=[r_small])
    sp.dma(smallv[:, 4:5], dn_d, writes=[r_small])
    base_mark = sb.mark()

    GW = O_LR + 32
    wg = sb.t([128, 8, GW], BF16)
    r_wg = Res()
    wdec = sb.t([17, 2, 512], BF16)
    r_wdec32, r_wdec = Res(), Res()
    p1_mark = sb.mark()
    wgst_r = Ring([sb.t([128, 8, 512], F32) for _ in range(2)])
    wdec32 = sb.t([17, 2, 512], F32)
    sp.dma(wdec32[:], wdec_d.rearrange("d k c -> k d c"), writes=[r_wdec32])
    pool.op(lambda e: e.tensor_copy(out=wdec[:], in_=wdec32[:]), reads=[r_wdec32], writes=[r_wdec])

    def load_wg_block(blk):
        c0 = blk * 512
        n = min(512, GW - c0)
        st, r_st = wgst_r.next()
        sp.dma(st[:, :, 0:n], win_d[:, c0:c0 + n].rearrange("(j p) c -> p j c", p=128), writes=[r_st])
        pool.op(lambda e: e.tensor_copy(out=wg[:, :, c0:c0 + n], in_=st[:, :, 0:n]), reads=[r_st], writes=[r_wg])

    cs32 = sb.t([128, 8, 2], F32)
    csb = sb.t([128, 8, 2], BF16)
    scb = sb.t([128, 8, 128], BF16)
    bada = sb.t([128, 24], F32)
    ones32 = sb.t([128, 128], F32)
    r_cs, r_csb, r_scb, r_bada, r_ones32 = Res(), Res(), Res(), Res(), Res()
    sp.dma(cs32[:], cvec_d, writes=[r_cs])
    sp.dma(bada[:], bada_d, writes=[r_bada])
    sp.dma(gxb[:], bass.AP(badag_d.tensor, 0, [[0, 128], [1, D]]), writes=[r_gxb])
    act.op(lambda e: e.activation(out=cs32[:], in_=cs32[:], func=AF.Silu), reads=[r_cs], writes=[r_cs])
    dve.op(lambda e: e.tensor_copy(out=csb[:], in_=cs32[:]), reads=[r_cs], writes=[r_csb])
    dve.op(lambda e: e.memset(ones32[:], 1.0), writes=[r_ones32])
    for j in range(8):
        dve.op(lambda e, j=j: e.tensor_scalar(out=scb[:, j, :], in0=ones32[:], scalar1=cs32[:, j, 0:1],
                                              scalar2=None, op0=ALU.mult),
               reads=[r_ones32, r_cs], writes=[r_scb])
    wst = [sb.t([128, 8, 512], F32) for _ in range(2)]
    wbf = [sb.t([128, 8, 512], BF16) for _ in range(2)]
    wst_r = Ring(wst)
    wbf_r = Ring(wbf)
    mod_ps = banks[0]
    r_modps = Res(True)
    g_ps = [banks[1], banks[2]]
    r_gps = [Res(True), Res(True)]
    for blk in range(6):
        st, r_st = wst_r.next()
        wb, r_wb = wbf_r.next()
        sp.dma(st[:], wada_d[:, blk * 512:(blk + 1) * 512].rearrange("(j p) c -> p j c", p=128), writes=[r_st])
        (dve if blk % 2 == 0 else pool).op(lambda e: e.tensor_copy(out=wb[:], in_=st[:]), reads=[r_st], writes=[r_wb])
        for q in range(4):
            m = blk * 4 + q
            for j in range(8):
                pe.op(lambda e, j=j, q=q, m=m: e.matmul(
                    mod_ps[:, 2 * m:2 * m + 2], lhsT=wb[:, j, q * 128:(q + 1) * 128], rhs=csb[:, j, :],
                    start=(j == 0), stop=(j == 7)),
                    reads=[r_wb, r_csb], writes=[r_modps], inc=(j == 7))
        if blk >= 4:
            gp, r_gp = g_ps[blk - 4], r_gps[blk - 4]
            for j in range(8):
                pe.op(lambda e, j=j: e.matmul(gp[:, :], lhsT=scb[:, j, :], rhs=wb[:, j, :],
                                              start=(j == 0), stop=(j == 7)),
                      reads=[r_wb, r_scb], writes=[r_gp], inc=(j == 7))
    mps3 = mod_ps[:, 0:48].rearrange("p (m t) -> p m t", t=2)
    for t in range(2):
        dve.op(lambda e, t=t: e.tensor_tensor(out=modT[:, :, t], in0=mps3[:, :, t], in1=bada[:, :], op=ALU.add),
               reads=[r_modps, r_bada], writes=[r_mod])
    dve.op(lambda e: e.tensor_scalar(out=modT[:, 8:16, :], in0=modT[:, 8:16, :], scalar1=1.0, scalar2=None,
                                     op0=ALU.add), reads=[r_mod], writes=[r_mod])
    for i in range(2):
        dve.op(lambda e, i=i: e.tensor_tensor(out=gxb[:, i * 512:(i + 1) * 512], in0=g_ps[i][:, :],
                                              in1=gxb[:, i * 512:(i + 1) * 512], op=ALU.add),
               reads=[r_gps[i], r_gxb], writes=[r_gxb])

    xt_r = Ring([sb.t([128, D], F32) for _ in range(3)])
    xn_r = Ring([sb.t([128, D], BF16) for _ in range(2)])
    junk = sb.t([128, D], BF16)
    r_junk = Res()
    st_r = Ring([sb.t([128, 4], F32) for _ in range(4)])
    hg_r = Ring([sb.t([128, 8, 512], BF16) for _ in range(2)])
    tp_banks = [banks[3], banks[4], banks[5]]
    tp_r = Ring([b.bitcast(BF16) for b in tp_banks], psum=True)
    groups = [("ctx", 0, 256)] + [("x", g * 512, 512) for g in range(SEQ // 512)]
    tiles = []
    tok0 = 0
    for (src, r0, n) in groups:
        for i in range(n // 128):
            tiles.append((src, r0, n, i, tok0))
        tok0 += n
    hg_cur = [None, None]

    def stage_a(ti):
        src, r0, n, i, tk = tiles[ti]
        xt, r_xt = xt_r.next()
        srcap = (ctx_d if src == "ctx" else x_d)[r0 + i * 128:r0 + (i + 1) * 128, :]
        sp.dma(xt[:], srcap, writes=[r_xt])
        stt, r_stt = st_r.next()
        act.op(lambda e: e.activation(out=junk[:], in_=xt[:], func=AF.Square, accum_out=stt[:, 0:1]),
               reads=[r_xt], writes=[r_junk, r_stt])
        act.op(lambda e: e.activation(out=stt[:, 1:2], in_=stt[:, 0:1], func=AF.Sqrt, bias=EPS, scale=1.0 / D),
               reads=[r_stt], writes=[r_stt])
        dve.op(lambda e: e.reciprocal(out=stt[:, 2:3], in_=stt[:, 1:2]), reads=[r_stt], writes=[r_stt])
        xn, r_xn = xn_r.next()
        dve.op(lambda e: e.tensor_scalar(out=xn[:], in0=xt[:], scalar1=stt[:, 2:3], scalar2=None, op0=ALU.mult),
               reads=[r_xt, r_stt], writes=[r_xn])
        tp, r_tp = tp_r.next()
        for j in range(8):
            pe.op(lambda e, j=j: e.transpose(tp[:, j * 128:(j + 1) * 128], xn[:, j * 128:(j + 1) * 128],
                                             cb[:, C_ID, :]),
                  reads=[r_xn, r_cb], writes=[r_tp], inc=(j == 7))
        return tp, r_tp

    def stage_b(ti, tp, r_tp):
        src, r0, n, i, tk = tiles[ti]
        if i == 0:
            hg_cur[0], hg_cur[1] = hg_r.next()
        hg, r_hg = hg_cur
        mcol = 1 if src == "ctx" else 0
        for j in range(8):
            if j % 2 == 0:
                act.op(lambda e, j=j: e.activation(out=hg[:, j, i * 128:(i + 1) * 128],
                                                   in_=tp[:, j * 128:(j + 1) * 128], func=AF.Identity,
                                                   bias=modT[:, j, mcol:mcol + 1],
                                                   scale=modT[:, 8 + j, mcol:mcol + 1]),
                       reads=[r_tp, r_mod], writes=[r_hg])
            else:
                dve.op(lambda e, j=j: e.tensor_scalar(out=hg[:, j, i * 128:(i + 1) * 128],
                                                      in0=tp[:, j * 128:(j + 1) * 128],
                                                      scalar1=modT[:, 8 + j, mcol:mcol + 1],
                                                      scalar2=modT[:, j, mcol:mcol + 1],
                                                      op0=ALU.mult, op1=ALU.add),
                       reads=[r_tp, r_mod], writes=[r_hg])
        if i == n // 128 - 1:
            pool.dma(hT_s[:, :, tk:tk + n].rearrange("j p t -> p j t"), hg[:, :, 0:n], reads=[r_hg], writes=[r_hT])

    NWB = (GW + 511) // 512
    pend = stage_a(0)
    for ti in range(len(tiles)):
        nxt = stage_a(ti + 1) if ti + 1 < len(tiles) else None
        stage_b(ti, *pend)
        pend = nxt
        if ti % 8 == 4 and ti // 8 < NWB:
            load_wg_block(ti // 8)
    fw.barrier()
    sb.reset(p1_mark)
    if dbg:
        d_mod = dout("d_mod", [128, 48])
        d_gxb = dout("d_gxb", [128, D])
        pool.dma(d_mod, modT[:].rearrange("p m t -> p (m t)"), reads=[r_mod])
        pool.dma(d_gxb, gxb[:], reads=[r_gxb])
        d_hT = dout("d_hT", [8, 128, NTOK], BF16)
        pool.dma(d_hT, hT_s, reads=[r_hT])
    def finish_zero():
        z = sb.t([128, D], F32)
        r_z = Res()
        r_out = Res()
        dve.op(lambda e: e.memset(z[:], 0.0), writes=[r_z])
        for i in range(OWN // 128):
            pool.dma(out_d[i * 128:(i + 1) * 128, :], z[:], reads=[r_z], writes=[r_out])
        fw.barrier()
        return nc, dbg_out

    if stop_after <= 0:
        return finish_zero()

    Sf_s = nc.dram_tensor("Sf_s", [NCH, 128, 1024], BF16).ap()
    r_Sf = [Res() for _ in range(NCH)]
    aT_s = nc.dram_tensor("aT_s", [8, 128, OWN], BF16).ap()
    r_aT = Res()
    pring = Ring(banks[0:6], psum=True)
    o_ps = [banks[6], banks[7]]
    r_ops = Res(True)
    hg_r = Ring([sb.t([128, 8, 512], BF16) for _ in range(2)])
    lrT_r = Ring([sb.t([32, 512], BF16) for _ in range(2)])
    for (t_, r_) in lrT_r.items:
        dve.op(lambda e: e.memset(t_[:], 1.0), writes=[r_])
    l_r = [Ring([sb.t([128, 4, 512], BF16) for _ in range(2)]) for _ in range(2)]
    etmp_r = Ring([sb.t([128, 512], F32) for _ in range(2)])
    ktm_r = Ring([sb.t([128, 4, 512], BF16) for _ in range(2)])
    vtm_r = Ring([sb.t([128, 4, 1024], BF16) for _ in range(2)])
    qT_r = Ring([sb.t([128, 4, 512], BF16) for _ in range(1)])
    kT_r = Ring([sb.t([128, 4, 512], BF16) for _ in range(1)])
    sg_r = Ring([sb.t([128, 8, 512], BF16) for _ in range(1)])
    eks_r = Ring([sb.t([128, 512], BF16) for _ in range(2)])
    ks_r = Ring([sb.t([128, 512], BF16) for _ in range(8)])
    dec_r = Ring([sb.t([128, 4], F32) for _ in range(8)])
    e_r = Ring([sb.t([128, 512], BF16) for _ in range(6)])
    qk_r = Ring([sb.t([128, 4, 128], BF16) for _ in range(4)])
    qi_r = Ring([sb.t([128, 4, 128], BF16) for _ in range(8)])
    AT_r = Ring([sb.t([128, 4, 128], BF16) for _ in range(8)])
    S32 = [sb.t([128, 4, 256], F32) for _ in range(2)]
    r_S32 = [Res(), Res()]
    Sb_bf = sb.t([128, 1024], BF16)
    r_Sbbf = Res()
    Sst_r = Ring([sb.t([128, 1024], BF16) for _ in range(2)])
    Sld_r = Ring([sb.t([128, 1024], BF16) for _ in range(2)])
    sq_r = Ring([sb.t([128, 8, 128], BF16) for _ in range(2)])
    rstd_r = Ring([sb.t([128, 8, 128], F32) for _ in range(1)])
    on_r = Ring([sb.t([128, 8, 128], F32) for _ in range(1)])
    aT_r = Ring([sb.t([128, 8, 128], BF16) for _ in range(2)])
    for d_ in range(2):
        dve.op(lambda e: e.memset(S32[d_][:], 0.0), writes=[r_S32[d_]])
    dve.op(lambda e: e.memset(Sb_bf[:], 0.0), writes=[r_Sbbf])
    cp_flip = [0]

    def evac(out_ap, in_ap, reads, writes):
        cp_flip[0] ^= 1
        if cp_flip[0]:
            act.op(lambda e: e.activation(out=out_ap, in_=in_ap, func=AF.Copy), reads=reads, writes=writes)
        else:
            dve.op(lambda e: e.tensor_copy(out=out_ap, in_=in_ap), reads=reads, writes=writes)

    def proj_fm(ps_ap, r_ps, c0, M, hg, r_hg, n):
        for j in range(8):
            pe.op(lambda e: e.matmul(ps_ap, lhsT=wg[:, j, c0:c0 + M], rhs=hg[:, j, 0:n],
                                     start=(j == 0), stop=(j == 7)),
                  reads=[r_wg, r_hg], writes=[r_ps], inc=(j == 7))

    def proj_tm(ps_ap, r_ps, c0, N, hg, r_hg, i):
        for j in range(8):
            pe.op(lambda e: e.matmul(ps_ap, lhsT=hg[:, j, i * 128:(i + 1) * 128], rhs=wg[:, j, c0:c0 + N],
                                     start=(j == 0), stop=(j == 7)),
                  reads=[r_wg, r_hg], writes=[r_ps], inc=(j == 7))

    def gla_group(tok0, nt, dirs, full):
        n = nt * 128
        hg, r_hg = hg_r.next()
        sp.dma(hg[:, :, 0:n], hT_s[:, :, tok0:tok0 + n].rearrange("j p t -> p j t"), reads=[r_hT], writes=[r_hg])
        G = {}
        for d_ in dirs:
            ps, r_ps = pring.next()
            proj_fm(ps[0:16, 0:n], r_ps, O_LR + 16 * d_, 16, hg, r_hg, n)
            lrT, r_lrT = lrT_r.next()
            evac(lrT[0:16, 0:n], ps[0:16, 0:n], [r_ps], [r_lrT])
            lt, r_lt = l_r[d_].next()
            for i in range(nt):
                ps, r_ps = pring.next()
                pe.op(lambda e: e.matmul(ps[:, :], lhsT=lrT[0:17, i * 128:(i + 1) * 128], rhs=wdec[0:17, d_, :],
                                         start=True, stop=True), reads=[r_lrT, r_wdec], writes=[r_ps])
                et, r_et = etmp_r.next()
                act.op(lambda e: e.activation(out=et[:], in_=ps[:, :], func=AF.Exp, scale=-1.0),
                       reads=[r_ps], writes=[r_et])
                act.op(lambda e: e.activation(out=lt[:, i, :], in_=et[:], func=AF.Ln, bias=1.0),
                       reads=[r_et], writes=[r_lt])
            G["l%d" % d_] = (lt, r_lt)
        ktm, r_ktm = ktm_r.next()
        vtm, r_vtm = vtm_r.next()
        for i in range(nt):
            ps, r_ps = pring.next()
            proj_tm(ps[:, :], r_ps, O_GK, 512, hg, r_hg, i)
            evac(ktm[:, i, :], ps[:, :], [r_ps], [r_ktm])
            for u in range(2):
                ps, r_ps = pring.next()
                proj_tm(ps[:, :], r_ps, O_GV + 512 * u, 512, hg, r_hg, i)
                evac(vtm[:, i, u * 512:(u + 1) * 512], ps[:, :], [r_ps], [r_vtm])
        G["k"] = (ktm, r_ktm)
        G["v"] = (vtm, r_vtm)
        if full:
            qT, r_qT = qT_r.next()
            kT, r_kT = kT_r.next()
            for h in range(4):
                ps, r_ps = pring.next()
                proj_fm(ps[:, 0:n], r_ps, O_GQ + 128 * h, 128, hg, r_hg, n)
                evac(qT[:, h, 0:n], ps[:, 0:n], [r_ps], [r_qT])
                ps, r_ps = pring.next()
                proj_fm(ps[:, 0:n], r_ps, O_GK + 128 * h, 128, hg, r_hg, n)
                evac(kT[:, h, 0:n], ps[:, 0:n], [r_ps], [r_kT])
            sg, r_sg = sg_r.next()
            for m in range(8):
                ps, r_ps = pring.next()
                proj_fm(ps[:, 0:n], r_ps, O_GG + 128 * m, 128, hg, r_hg, n)
                et, r_et = etmp_r.next()
                act.op(lambda e: e.activation(out=et[:, 0:n], in_=ps[:, 0:n], func=AF.Silu),
                       reads=[r_ps], writes=[r_et])
                dve.op(lambda e: e.tensor_scalar(out=sg[:, m, 0:n], in0=et[:, 0:n], scalar1=smallv[:, (m % 2):(m % 2) + 1],
                                                 scalar2=None, op0=ALU.mult),
                       reads=[r_et, r_small], writes=[r_sg])
            G["qT"] = (qT, r_qT)
            G["kT"] = (kT, r_kT)
            G["sg"] = (sg, r_sg)
        return G

    def state_pieces(d_, lt, r_lt, ktm, r_ktm, i):
        ps, r_ps = pring.next()
        pe.op(lambda e: e.matmul(ps[:, :], lhsT=cb[:, C_M1F + d_, :], rhs=lt[:, i, :], start=True, stop=True),
              reads=[r_cb, r_lt], writes=[r_ps])
        eks, r_eks = eks_r.next()
        act.op(lambda e: e.activation(out=eks[:], in_=ps[:, :], func=AF.Exp), reads=[r_ps], writes=[r_eks])
        ks, r_ks = ks_r.next()
        dve.op(lambda e: e.tensor_tensor(out=ks[:], in0=ktm[:, i, :], in1=eks[:], op=ALU.mult),
               reads=[r_ktm, r_eks], writes=[r_ks])
        ps, r_ps = pring.next()
        for h in range(4):
            pe.op(lambda e: e.matmul(ps[:, h:h + 1], lhsT=lt[:, i, h * 128:(h + 1) * 128], rhs=cb[:, C_TRIF, 127:128],
                                     start=True, stop=True), reads=[r_cb, r_lt], writes=[r_ps], inc=(h == 3))
        dec, r_dec = dec_r.next()
        act.op(lambda e: e.activation(out=dec[:], in_=ps[:, 0:4], func=AF.Exp), reads=[r_ps], writes=[r_dec])
        return ks, r_ks, dec, r_dec

    def state_update(d_, ks, r_ks, dec, r_dec, vtm, r_vtm, i):
        S, r_S = S32[d_], r_S32[d_]
        for hp in range(2):
            ps, r_ps = pring.next()
            for hh in range(2):
                h = hp * 2 + hh
                pe.op(lambda e: e.matmul(ps[:, hh * 256:(hh + 1) * 256], lhsT=ks[:, h * 128:(h + 1) * 128],
                                         rhs=vtm[:, i, h * 256:(h + 1) * 256], start=True, stop=True),
                      reads=[r_ks, r_vtm], writes=[r_ps], inc=(hh == 1))
            for hh in range(2):
                h = hp * 2 + hh
                dve.op(lambda e: e.scalar_tensor_tensor(out=S[:, h, :], in0=S[:, h, :], scalar=dec[:, h:h + 1],
                                                        in1=ps[:, hh * 256:(hh + 1) * 256], op0=ALU.mult, op1=ALU.add),
                       reads=[r_S, r_dec, r_ps], writes=[r_S])

    G = gla_group(0, 2, (0, 1), False)
    for i in (0, 1):
        ks, r_ks, dec, r_dec = state_pieces(0, *G["l0"], *G["k"], i)
        state_update(0, ks, r_ks, dec, r_dec, *G["v"], i)
    for i in (1, 0):
        ks, r_ks, dec, r_dec = state_pieces(1, *G["l1"], *G["k"], i)
        state_update(1, ks, r_ks, dec, r_dec, *G["v"], i)

    def prep_state(tok0, d_, order):
        G_ = gla_group(tok0, 4, (d_,), False)
        pcs = {i: state_pieces(d_, *G_["l%d" % d_], *G_["k"], i) for i in order}
        return G_, pcs

    jobs = [(CTX + g * 512, 0, (0, 1, 2, 3), g) for g in range(8)]
    jobs += [(CTX + g * 512, 1, (3, 2, 1, 0), None) for g in range(15, 7, -1)]
    nxt = prep_state(*jobs[0][0:3])
    for ji, (tok0_, d_, order, gown) in enumerate(jobs):
        G, pcs = nxt
        nxt = prep_state(*jobs[ji + 1][0:3]) if ji + 1 < len(jobs) else None
        for i in order:
            if gown is not None:
                c = gown * 4 + i
                sst, r_sst = Sst_r.next()
                act.op(lambda e: e.activation(out=sst[:], in_=S32[0][:].rearrange("p h v -> p (h v)"), func=AF.Copy),
                       reads=[r_S32[0]], writes=[r_sst])
                pool.dma(Sf_s[c], sst[:], reads=[r_sst], writes=[r_Sf[c]])
            ks, r_ks, dec, r_dec = pcs[i]
            state_update(d_, ks, r_ks, dec, r_dec, *G["v"], i)
    act.op(lambda e: e.activation(out=Sb_bf[:], in_=S32[1][:].rearrange("p h v -> p (h v)"), func=AF.Copy),
           reads=[r_S32[1]], writes=[r_Sbbf])
    for g in range(7, -1, -1):
        G = gla_group(CTX + g * 512, 4, (0, 1), True)
        qT, r_qT = G["qT"]
        kT, r_kT = G["kT"]
        sg, r_sg = G["sg"]
        vtm, r_vtm = G["v"]
        prepd = {}
        for i in range(3, -1, -1):
            sl = slice(i * 128, (i + 1) * 128)
            per = []
            for d_ in range(2):
                lt, r_lt = G["l%d" % d_]
                psA, r_psA = pring.next()
                psB, r_psB = pring.next()
                for h in range(4):
                    pe.op(lambda e: e.matmul(psA[:, h * 128:(h + 1) * 128], lhsT=lt[:, i, h * 128:(h + 1) * 128],
                                             rhs=cb[:, C_RDF + d_, :], start=True, stop=True),
                          reads=[r_lt, r_cb], writes=[r_psA], inc=(h == 3))
                for h in range(4):
                    pe.op(lambda e: e.matmul(psB[:, h * 128:(h + 1) * 128], lhsT=lt[:, i, h * 128:(h + 1) * 128],
                                             rhs=cb[:, C_TRIF + d_, :], start=True, stop=True),
                          reads=[r_lt, r_cb], writes=[r_psB], inc=(h == 3))
                e1, r_e1 = e_r.next()
                e2, r_e2 = e_r.next()
                e3, r_e3 = e_r.next()
                act.op(lambda e: e.activation(out=e1[:], in_=psA[:, :], func=AF.Exp, scale=-1.0, bias=LN_QS),
                       reads=[r_psA], writes=[r_e1])
                act.op(lambda e: e.activation(out=e2[:], in_=psA[:, :], func=AF.Exp), reads=[r_psA], writes=[r_e2])
                act.op(lambda e: e.activation(out=e3[:], in_=psB[:, :], func=AF.Exp, bias=LN_QS),
                       reads=[r_psB], writes=[r_e3])
                qs, r_qs = qk_r.next()
                kk, r_kk = qk_r.next()
                qi, r_qi = qi_r.next()
                dve.op(lambda e: e.tensor_tensor(out=qs[:], in0=qT[:, :, sl],
                                                 in1=e1[:].rearrange("p (h t) -> p h t", h=4), op=ALU.mult),
                       reads=[r_qT, r_e1], writes=[r_qs])
                dve.op(lambda e: e.tensor_tensor(out=kk[:], in0=kT[:, :, sl],
                                                 in1=e2[:].rearrange("p (h t) -> p h t", h=4), op=ALU.mult),
                       reads=[r_kT, r_e2], writes=[r_kk])
                pool.op(lambda e: e.tensor_tensor(out=qi[:], in0=qT[:, :, sl],
                                                  in1=e3[:].rearrange("p (h t) -> p h t", h=4), op=ALU.mult),
                        reads=[r_qT, r_e3], writes=[r_qi])
                psS, r_psS = pring.next()
                for h in range(4):
                    pe.op(lambda e: e.matmul(psS[:, h * 128:(h + 1) * 128], lhsT=kk[:, h, :], rhs=qs[:, h, :],
                                             start=True, stop=True),
                          reads=[r_kk, r_qs], writes=[r_psS], inc=(h == 3))
                AT, r_AT = AT_r.next()
                for h in range(4):
                    dve.op(lambda e: e.tensor_tensor(out=AT[:, h, :], in0=psS[:, h * 128:(h + 1) * 128],
                                                     in1=cb[:, C_MSKF + d_, :], op=ALU.mult),
                           reads=[r_psS, r_cb], writes=[r_AT])
                per.append((AT, r_AT, qi, r_qi))
            prepd[i] = (per, state_pieces(1, *G["l1"], *G["k"], i))
        for i in range(3, -1, -1):
            c = g * 4 + i
            sl = slice(i * 128, (i + 1) * 128)
            sld, r_sld = Sld_r.next()
            sp.dma(sld[:], Sf_s[c], reads=[r_Sf[c]], writes=[r_sld])
            per, (ks, r_ks, dec, r_dec) = prepd[i]
            (ATf, r_ATf, qif, r_qif), (ATb, r_ATb, qib, r_qib) = per
            for h in range(4):
                for u in range(2):
                    m = h * 2 + u
                    o_ap = o_ps[m // 4][:, (m % 4) * 128:(m % 4 + 1) * 128]
                    vs = slice(h * 256 + u * 128, h * 256 + u * 128 + 128)
                    pe.op(lambda e: e.matmul(o_ap, lhsT=vtm[:, i, vs], rhs=ATf[:, h, :], start=True, stop=False),
                          reads=[r_vtm, r_ATf], writes=[r_ops], inc=False)
                    pe.op(lambda e: e.matmul(o_ap, lhsT=vtm[:, i, vs], rhs=ATb[:, h, :], start=False, stop=False),
                          reads=[r_vtm, r_ATb], writes=[r_ops], inc=False)
                    pe.op(lambda e: e.matmul(o_ap, lhsT=sld[:, vs], rhs=qif[:, h, :], start=False, stop=False),
                          reads=[r_sld, r_qif], writes=[r_ops], inc=False)
                    pe.op(lambda e: e.matmul(o_ap, lhsT=Sb_bf[:, vs], rhs=qib[:, h, :], start=False, stop=True),
                          reads=[r_Sbbf, r_qib], writes=[r_ops], inc=(m == 7))
            state_update(1, ks, r_ks, dec, r_dec, vtm, r_vtm, i)
            act.op(lambda e: e.activation(out=Sb_bf[:], in_=S32[1][:].rearrange("p h v -> p (h v)"), func=AF.Copy),
                   reads=[r_S32[1]], writes=[r_Sbbf])
            on, r_on = on_r.next()
            sq, r_sq = sq_r.next()
            for bnk in range(2):
                dve.op(lambda e: e.tensor_copy(out=on[:, bnk * 4:(bnk + 1) * 4, :].rearrange("p m t -> p (m t)"),
                                               in_=o_ps[bnk][:, :]), reads=[r_ops], writes=[r_on])
            act.op(lambda e: e.activation(out=sq[:], in_=on[:], func=AF.Square), reads=[r_on], writes=[r_sq])
            ss = [pring.next(), pring.next()]
            for h in range(4):
                for u2 in range(2):
                    m = h * 2 + u2
                    ps, r_ps = ss[m // 4]
                    for u in range(2):
                        pe.op(lambda e: e.matmul(ps[:, (m % 4) * 128:(m % 4 + 1) * 128], lhsT=cb[:, C_ONES, :],
                                                 rhs=sq[:, h * 2 + u, :], start=(u == 0), stop=(u == 1)),
                              reads=[r_cb, r_sq], writes=[r_ps], inc=(u == 1 and m % 4 == 3))
            rstd, r_rstd = rstd_r.next()
            for bnk in range(2):
                ps, r_ps = ss[bnk]
                act.op(lambda e: e.activation(out=rstd[:, bnk * 4:(bnk + 1) * 4, :].rearrange("p m t -> p (m t)"),
                                              in_=ps[:, :], func=AF.Ln, scale=1.0 / 256.0, bias=EPS),
                       reads=[r_ps], writes=[r_rstd])
            act.op(lambda e: e.activation(out=rstd[:], in_=rstd[:], func=AF.Exp, scale=-0.5),
                   reads=[r_rstd], writes=[r_rstd])
            dve.op(lambda e: e.tensor_tensor(out=on[:], in0=on[:], in1=rstd[:], op=ALU.mult),
                   reads=[r_on, r_rstd], writes=[r_on])
            aT, r_aTt = aT_r.next()
            pool.op(lambda e: e.tensor_tensor(out=aT[:], in0=on[:], in1=sg[:, :, sl], op=ALU.mult),
                    reads=[r_on, r_sg], writes=[r_aTt])
            pool.dma(aT_s[:, :, c * 128:(c + 1) * 128].rearrange("m p t -> p m t"), aT[:], reads=[r_aTt], writes=[r_aT])
    fw.barrier()
    sb.reset(base_mark)
    if dbg:
        d_aT = dout("d_aT", [8, 128, OWN], BF16)
        pool.dma(d_aT, aT_s, reads=[r_aT])
        d_Sf = dout("d_Sf", [NCH, 128, 1024], BF16)
        pool.dma(d_Sf, Sf_s, reads=r_Sf)
    if stop_after <= 1:
        return finish_zero()

    bT_s = nc.dram_tensor("bT_s", [8, 128, OWN], BF16).ap()
    r_bT = Res()
    lamt = sb.t([128, 256], F32)
    qkrow = sb.t([128, 128], F32)
    scal = sb.t([128, 16], F32)
    tmpq = sb.t([128, 128], F32)
    r_lamt, r_qkrow, r_scal, r_tmpq = Res(), Res(), Res(), Res()
    sp.dma(lamt[:], bass.AP(lam_d.tensor, 0, [[0, 128], [1, 256]]), writes=[r_lamt])
    sp.dma(qkrow[:], bass.AP(qkrow_d.tensor, 0, [[0, 128], [1, 128]]), writes=[r_qkrow])
    dve.op(lambda e: e.tensor_tensor(out=tmpq[:, 0:64], in0=lamt[:, 0:64], in1=lamt[:, 64:128], op=ALU.mult),
           reads=[r_lamt], writes=[r_tmpq])
    dve.op(lambda e: e.tensor_tensor(out=tmpq[:, 64:128], in0=lamt[:, 128:192], in1=lamt[:, 192:256], op=ALU.mult),
           reads=[r_lamt, r_tmpq], writes=[r_tmpq])
    dve.op(lambda e: e.reduce_sum(out=scal[:, 0:1], in_=tmpq[:, 0:64], axis=AX.X), reads=[r_tmpq], writes=[r_scal])
    dve.op(lambda e: e.reduce_sum(out=scal[:, 1:2], in_=tmpq[:, 64:128], axis=AX.X), reads=[r_tmpq, r_scal], writes=[r_scal])
    act.op(lambda e: e.activation(out=scal[:, 0:2], in_=scal[:, 0:2], func=AF.Exp), reads=[r_scal], writes=[r_scal])
    dve.op(lambda e: e.scalar_tensor_tensor(out=scal[:, 2:3], in0=scal[:, 1:2], scalar=-LAM_INIT, in1=scal[:, 0:1],
                                            op0=ALU.add, op1=ALU.subtract), reads=[r_scal], writes=[r_scal])
    dve.op(lambda e: e.tensor_scalar(out=tmpq[:], in0=qkrow[:], scalar1=-1.0, scalar2=None, op0=ALU.mult),
           reads=[r_qkrow, r_tmpq], writes=[r_tmpq])
    dve.op(lambda e: e.tensor_tensor(out=tmpq[:], in0=tmpq[:], in1=qkrow[:], op=ALU.max),
           reads=[r_qkrow, r_tmpq], writes=[r_tmpq])
    dve.op(lambda e: e.reduce_max(out=scal[:, 3:4], in_=tmpq[:, 0:64], axis=AX.X), reads=[r_tmpq, r_scal], writes=[r_scal])
    dve.op(lambda e: e.reduce_max(out=scal[:, 4:5], in_=tmpq[:, 64:128], axis=AX.X), reads=[r_tmpq, r_scal], writes=[r_scal])
    dve.op(lambda e: e.scalar_tensor_tensor(out=scal[:, 5:6], in0=scal[:, 3:4], scalar=-8.0, in1=scal[:, 4:5],
                                            op0=ALU.mult, op1=ALU.mult), reads=[r_scal], writes=[r_scal])
    dve.op(lambda e: e.tensor_scalar(out=scal[:, 6:7], in0=smallv[:, 4:5], scalar1=(1.0 - LAM_INIT), scalar2=None,
                                     op0=ALU.mult), reads=[r_small, r_scal], writes=[r_scal])
    dve.op(lambda e: e.tensor_scalar(out=scal[:, 7:8], in0=smallv[:, 2:3], scalar1=0.125, scalar2=None,
                                     op0=ALU.mult), reads=[r_small, r_scal], writes=[r_scal])
    ones_bf = sb.t([128, 32], BF16)
    r_onesbf = Res()
    dve.op(lambda e: e.memset(ones_bf[:], 1.0), writes=[r_onesbf])
    sel = sb.t([64, 2, 128], BF16)
    r_sel = Res()
    dve.op(lambda e: e.memset(sel[:], 0.0), writes=[r_sel])
    dve.op(lambda e: e.memset(sel[0:1, 0, :], 1.0), writes=[r_sel])
    dve.op(lambda e: e.memset(sel[32:33, 1, :], 1.0), writes=[r_sel])

    if cut == 1:
        fw.barrier()
        return finish_zero()
    wst_r = Ring([sb.t([128, 8, 128], F32) for _ in range(3)])
    wh_r = Ring([sb.t([128, 8, 512], BF16) for _ in range(2)])
    KT = sb.t([128, NTOK], BF16)
    Vt = sb.t([128, NTOK // 128, 128], BF16)
    QT = sb.t([128, OWN], BF16)
    sgT = sb.t([128, OWN], BF16)
    gpre = sb.t([128, OWN], F32)
    r_KT, r_V, r_QT, r_sgT, r_gpre = Res(), Res(), Res(), Res(), Res()
    hg_r = Ring([sb.t([128, 8, 512], BF16) for _ in range(2)])
    cos_r = Ring([sb.t([128, 512], F32) for _ in range(2)])
    sin_r = Ring([sb.t([128, 512], F32) for _ in range(2)])
    sqb_r = Ring([sb.t([128, 512], BF16) for _ in range(2)])
    kgb_r = Ring([sb.t([128, 512], BF16) for _ in range(2)])
    f1_r = Ring([sb.t([128, 512], F32) for _ in range(2)])
    f2_r = Ring([sb.t([128, 512], F32) for _ in range(2)])
    f3_r = Ring([sb.t([128, 512], F32) for _ in range(2)])
    P_r = Ring([sb.t([128, 1024], BF16) for _ in range(4)])
    rl_r = Ring([sb.t([64, 512], F32) for _ in range(1)])
    rh_r = Ring([sb.t([64, 2, 512], BF16) for _ in range(1)])
    o1_r = Ring([sb.t([128, 512], F32) for _ in range(1)])
    o2_r = Ring([sb.t([128, 512], F32) for _ in range(1)])
    bo_r = Ring([sb.t([128, 512], BF16) for _ in range(2)])
    bres = [Res(True) for _ in range(8)]
    s_ring = Ring([pairs[0], pairs[1]], res=[(bres[0], bres[1]), (bres[2], bres[3])])
    O1, O2, Lb, Mb = banks[4], banks[5], banks[6], banks[7]
    r_O1, r_O2, r_L, r_M = bres[4], bres[5], bres[6], bres[7]
    bring = Ring(banks[0:4], res=bres[0:4])
    mring = Ring(banks[4:8], res=bres[4:8])

    def rope_norm(ps, r_ps, n, gain_ap, r_gain, cosg, r_cos, sing, r_sin, out_ap, r_out):
        sq, r_sq = sqb_r.next()
        kg, r_kg = kgb_r.next()
        act.op(lambda e: e.activation(out=sq[:, 0:n], in_=ps[:, 0:n], func=AF.Square), reads=[r_ps], writes=[r_sq])
        act.op(lambda e: e.activation(out=kg[:, 0:n], in_=ps[:, 0:n], func=AF.Copy, scale=gain_ap),
               reads=[r_ps, r_gain], writes=[r_kg])
        p2, r_p2 = mring.next()
        p3, r_p3 = mring.next()
        pe.op(lambda e: e.matmul(p2[:, 0:n], lhsT=cb[:, C_BLK64, :], rhs=sq[:, 0:n], start=True, stop=True),
              reads=[r_cb, r_sq], writes=[r_p2])
        pe.op(lambda e: e.matmul(p3[:, 0:n], lhsT=cb[:, C_PERM, :], rhs=kg[:, 0:n], start=True, stop=True),
              reads=[r_cb, r_kg], writes=[r_p3])
        f3, r_f3 = f3_r.next()
        act.op(lambda e: e.activation(out=f3[:, 0:n], in_=p2[:, 0:n], func=AF.Ln, scale=1.0 / 64.0, bias=EPS),
               reads=[r_p2], writes=[r_f3])
        act.op(lambda e: e.activation(out=f3[:, 0:n], in_=f3[:, 0:n], func=AF.Exp, scale=-0.5),
               reads=[r_f3], writes=[r_f3])
        f1, r_f1 = f1_r.next()
        f2, r_f2 = f2_r.next()
        dve.op(lambda e: e.scalar_tensor_tensor(out=f1[:, 0:n], in0=ps[:, 0:n], scalar=gain_ap,
                                                in1=cosg[:, 0:n], op0=ALU.mult, op1=ALU.mult),
               reads=[r_ps, r_gain, r_cos], writes=[r_f1])
        dve.op(lambda e: e.tensor_tensor(out=f2[:, 0:n], in0=p3[:, 0:n], in1=sing[:, 0:n], op=ALU.mult),
               reads=[r_p3, r_sin], writes=[r_f2])
        dve.op(lambda e: e.tensor_tensor(out=f1[:, 0:n], in0=f1[:, 0:n], in1=f2[:, 0:n], op=ALU.add),
               reads=[r_f1, r_f2], writes=[r_f1])
        pool.op(lambda e: e.tensor_tensor(out=out_ap, in0=f1[:, 0:n], in1=f3[:, 0:n], op=ALU.mult),
                reads=[r_f1, r_f3], writes=[r_out])

    groups2 = [(0, 256)] + [(CTX + g * 512, 512) for g in range(SEQ // 512)]
    sqp = sb.t([128, 512], BF16)
    r_sqp = Res()
    carry = None
    for h in range(NHEADS_RUN):
        wh, r_wh = wh_r.next()
        for si, off in enumerate((O_DQ, O_DK, O_DV, O_DG)):
            st, r_st = wst_r.next()
            sp.dma(st[:], win_d[:, off + h * 128:off + (h + 1) * 128].rearrange("(j p) c -> p j c", p=128), writes=[r_st])
            (dve if si % 2 == 0 else pool).op(
                lambda e: e.tensor_copy(out=wh[:, :, si * 128:(si + 1) * 128], in_=st[:]), reads=[r_st], writes=[r_wh])
        if cut == 2:
            fw.barrier()
            return finish_zero()
        for gi, (tok0, n) in enumerate(groups2):
            if cut == 3 and gi == 1:
                fw.barrier()
                return finish_zero()
            if cut == 4 and gi == 2:
                fw.barrier()
                return finish_zero()
            hg, r_hg = hg_r.next()
            sp.dma(hg[:, :, 0:n], hT_s[:, :, tok0:tok0 + n].rearrange("j p t -> p j t"), reads=[r_hT], writes=[r_hg])
            cosg, r_cos = cos_r.next()
            sing, r_sin = sin_r.next()
            sp.dma(cosg[:, 0:n], cos_d[:, tok0:tok0 + n], writes=[r_cos])
            sp.dma(sing[:, 0:n], sin_d[:, tok0:tok0 + n], writes=[r_sin])
            own = 1 <= gi <= 8
            nt = n // 128
            psK, r_psK = bring.next()
            for j in range(8):
                pe.op(lambda e: e.matmul(psK[:, 0:n], lhsT=wh[:, j, 128:256], rhs=hg[:, j, 0:n], start=(j == 0), stop=(j == 7)),
                      reads=[r_wh, r_hg], writes=[r_psK], inc=(j == 7))
            psV, r_psV = bring.next()
            for i in range(nt):
                for j in range(8):
                    pe.op(lambda e: e.matmul(psV[:, i * 128:(i + 1) * 128], lhsT=hg[:, j, i * 128:(i + 1) * 128],
                                             rhs=wh[:, j, 256:384], start=(j == 0), stop=(j == 7)),
                          reads=[r_wh, r_hg], writes=[r_psV], inc=(j == 7 and i == nt - 1))
            if own:
                osl = slice((gi - 1) * 512, gi * 512)
                psQ, r_psQ = bring.next()
                for j in range(8):
                    pe.op(lambda e: e.matmul(psQ[:, :], lhsT=wh[:, j, 0:128], rhs=hg[:, j, :], start=(j == 0), stop=(j == 7)),
                          reads=[r_wh, r_hg], writes=[r_psQ], inc=(j == 7))
                psG, r_psG = bring.next()
                for j in range(8):
                    pe.op(lambda e: e.matmul(psG[:, :], lhsT=wh[:, j, 384:512], rhs=hg[:, j, :], start=(j == 0), stop=(j == 7)),
                          reads=[r_wh, r_hg], writes=[r_psG], inc=(j == 7))
            t0 = tok0 // 128
            dve.op(lambda e: e.tensor_copy(out=Vt[:, t0:t0 + nt, :].rearrange("p a b -> p (a b)"), in_=psV[:, 0:n]),
                   reads=[r_psV], writes=[r_V])
            if carry is not None and 1 <= gi <= 5:
                for stg in {1: (0,), 2: (1,), 3: (2,), 4: (3, 4), 5: (5,)}[gi]:
                    carry[0](carry[1], stg)
                if gi == 5:
                    carry = None
            rope_norm(psK, r_psK, n, smallv[:, 3:4], r_small, cosg, r_cos, sing, r_sin, KT[:, tok0:tok0 + n], r_KT)
            if own:
                rope_norm(psQ, r_psQ, 512, scal[:, 7:8], r_scal, cosg, r_cos, sing, r_sin, QT[:, osl], r_QT)
                act.op(lambda e: e.activation(out=gpre[:, osl], in_=psG[:, :], func=AF.Copy), reads=[r_psG], writes=[r_gpre])
        act.op(lambda e: e.activation(out=gpre[:], in_=gpre[:], func=AF.Silu), reads=[r_gpre], writes=[r_gpre])
        dve.op(lambda e: e.tensor_scalar(out=sgT[:], in0=gpre[:], scalar1=scal[:, 6:7], scalar2=None, op0=ALU.mult),
               reads=[r_gpre, r_scal], writes=[r_sgT])
        NKT = NTOK // 128
        steps = [(qg, kt) for qg in range(NQG_RUN) for kt in range(NKT)]
        sbuf_S = {}

        def emit_S(i):
            qg, kt = steps[i]
            qsl = slice(qg * 512, (qg + 1) * 512)
            ksl = slice(kt * 128, (kt + 1) * 128)
            sp_, r_S = s_ring.next()
            pe.op(lambda e: e.matmul(sp_[:, 0, :], lhsT=KT[0:64, ksl], rhs=QT[0:64, qsl], start=True, stop=True),
                  reads=[r_KT, r_QT], writes=[*r_S], inc=False)
            pe.op(lambda e: e.matmul(sp_[:, 1, :], lhsT=KT[64:128, ksl], rhs=QT[64:128, qsl], start=True, stop=True,
                                     tile_position=(64, 0)),
                  reads=[r_KT, r_QT], writes=[*r_S])
            sbuf_S[i] = (sp_, r_S)

        sbuf_P = {}
        pendL = []
        assert NKT % 2 == 0

        def emit_exp(i):
            sp_, r_S = sbuf_S.pop(i)
            P, r_P = P_r.next()
            act.op(lambda e: e.activation(out=P[:], in_=sp_[:].rearrange("p a b -> p (a b)"), func=AF.Exp,
                                          bias=scal[:, 5:6]),
                   reads=[*r_S, r_scal], writes=[r_P])
            sbuf_P[i] = (P, r_P)

        def emit_AV(i):
            qg, kt = steps[i]
            P, r_P = sbuf_P.pop(i)
            first, last = (kt == 0), (kt == NKT - 1)
            pe.op(lambda e: e.matmul(O1[:, :], lhsT=Vt[:, kt, :], rhs=P[:, 0:512], start=first, stop=last),
                  reads=[r_V, r_P], writes=[r_O1], inc=False)
            pe.op(lambda e: e.matmul(O2[:, :], lhsT=Vt[:, kt, :], rhs=P[:, 512:1024], start=first, stop=last),
                  reads=[r_V, r_P], writes=[r_O2])
            pendL.append((kt, P, r_P))
            if kt % 2 == 1:
                for n_, (kt_, P_, r_P_) in enumerate(pendL):
                    f_, l_ = (kt_ == 0), (kt_ == NKT - 1)
                    pe.op(lambda e: e.matmul(Lb[0:32, :], lhsT=ones_bf[:, :], rhs=P_[:, 0:512], start=f_, stop=l_,
                                             tile_position=(0, 0)),
                          reads=[r_onesbf, r_P_], writes=[r_L], inc=False)
                    pe.op(lambda e: e.matmul(Lb[32:64, :], lhsT=ones_bf[:, :], rhs=P_[:, 512:1024], start=f_, stop=l_,
                                             tile_position=(0, 32)),
                          reads=[r_onesbf, r_P_], writes=[r_L], inc=(n_ == len(pendL) - 1))
                pendL.clear()

        def post_a(qg):
            rl, r_rl = rl_r.next()
            o1, r_o1 = o1_r.next()
            o2, r_o2 = o2_r.next()
            rh, r_rh = rh_r.next()
            dve.op(lambda e: e.tensor_copy(out=rl[:], in_=Lb[0:64, :]), reads=[r_L], writes=[r_rl])
            dve.op(lambda e: e.tensor_copy(out=o1[:], in_=O1[:, :]), reads=[r_O1], writes=[r_o1])
            dve.op(lambda e: e.tensor_copy(out=o2[:], in_=O2[:, :]), reads=[r_O2], writes=[r_o2])
            dve.op(lambda e: e.reciprocal(out=rl[:], in_=rl[:]), reads=[r_rl], writes=[r_rl])
            dve.op(lambda e: e.tensor_scalar(out=rl[32:64, :], in0=rl[32:64, :], scalar1=scal[32:64, 2:3], scalar2=None,
                                             op0=ALU.mult), reads=[r_rl, r_scal], writes=[r_rl])
            dve.op(lambda e: e.tensor_copy(out=rh[:, 0, :], in_=rl[:]), reads=[r_rl], writes=[r_rh])
            dve.op(lambda e: e.tensor_tensor(out=rh[:, 1, :], in0=rl[:], in1=rh[:, 0, :], op=ALU.subtract),
                   reads=[r_rl, r_rh], writes=[r_rh])
            return dict(qg=qg, rl=(rl, r_rl), o1=(o1, r_o1), o2=(o2, r_o2), rh=(rh, r_rh))

        def post_b(st, stage, hh=h):
            o1, r_o1 = st["o1"]
            o2, r_o2 = st["o2"]
            rh, r_rh = st["rh"]
            qsl = slice(st["qg"] * 512, (st["qg"] + 1) * 512)
            if stage in (0, 1):
                osb, r_osb = (o1, r_o1) if stage == 0 else (o2, r_o2)
                for part in range(2):
                    pe.op(lambda e: e.matmul(Mb[:, :], lhsT=sel[:, stage, :], rhs=rh[:, part, :],
                                             start=(part == 0), stop=(part == 1)),
                          reads=[r_sel, r_rh], writes=[r_M], inc=(part == 1))
                dve.op(lambda e: e.tensor_tensor(out=osb[:], in0=osb[:], in1=Mb[:, :], op=ALU.mult),
                       reads=[r_osb, r_M], writes=[r_osb])
            elif stage == 2:
                dve.op(lambda e: e.tensor_tensor(out=o1[:], in0=o1[:], in1=o2[:], op=ALU.add),
                       reads=[r_o1, r_o2], writes=[r_o1])
                sq, r_sq = sqp, r_sqp
                st["sq"] = (sq, r_sq)
                dve.op(lambda e: e.tensor_tensor(out=sq[:], in0=o1[:], in1=o1[:], op=ALU.mult), reads=[r_o1], writes=[r_sq])
            elif stage == 3:
                sq, r_sq = st["sq"]
                pe.op(lambda e: e.matmul(Mb[:, :], lhsT=cb[:, C_ONES, :], rhs=sq[:], start=True, stop=True),
                      reads=[r_cb, r_sq], writes=[r_M])
            elif stage == 4:
                act.op(lambda e: e.activation(out=o2[:], in_=Mb[:, :], func=AF.Ln, scale=1.0 / 128.0, bias=EPS),
                       reads=[r_M, r_o2], writes=[r_o2])
                act.op(lambda e: e.activation(out=o2[:], in_=o2[:], func=AF.Exp, scale=-0.5), reads=[r_o2], writes=[r_o2])
            elif stage == 5:
                dve.op(lambda e: e.tensor_tensor(out=o1[:], in0=o1[:], in1=o2[:], op=ALU.mult),
                       reads=[r_o1, r_o2], writes=[r_o1])
                bo, r_bo = bo_r.next()
                dve.op(lambda e: e.tensor_tensor(out=bo[:], in0=o1[:], in1=sgT[:, qsl], op=ALU.mult),
                       reads=[r_o1, r_sgT], writes=[r_bo])
                pool.dma(bT_s[hh][:, qsl], bo[:], reads=[r_bo], writes=[r_bT])

        POST_AT = {8: 0, 11: 1, 14: 2, 17: 3, 20: 4, 23: 5}
        for i0 in range(min(2, len(steps))):
            emit_S(i0)
        pending = None
        for i, (qg, kt) in enumerate(steps):
            emit_exp(i)
            if i + 2 < len(steps):
                emit_S(i + 2)
            emit_AV(i)
            if pending is not None and kt in POST_AT:
                post_b(pending, POST_AT[kt])
                if POST_AT[kt] == 5:
                    pending = None
            if kt == NKT - 1:
                pending = post_a(qg)
                if qg == NQG_RUN - 1:
                    if h == NHEADS_RUN - 1:
                        for stg in range(6):
                            post_b(pending, stg)
                    else:
                        carry = (post_b, pending)
                    pending = None
    fw.barrier()
    sb.reset(base_mark)
    if dbg:
        d_bT = dout("d_bT", [8, 128, OWN], BF16)
        if NQG_RUN > 0:
            pool.dma(d_bT[0:NHEADS_RUN, :, 0:NQG_RUN * 512], bT_s[0:NHEADS_RUN, :, 0:NQG_RUN * 512], reads=[r_bT])
    if stop_after <= 2:
        return finish_zero()

    wm = sb.t([128, 8, 2048], BF16)
    wbg = sb.t([128, 8, D], BF16)
    wbd = sb.t([128, 8, D], BF16)
    wo = sb.t([128, 8, D], BF16)
    r_wm, r_wbg, r_wbd, r_wo = Res(), Res(), Res(), Res()
    wst_r = Ring([sb.t([128, 8, 256], F32) for _ in range(2)])
    loads = [(wm, r_wm, win_d, O_MG + c * 256, c * 256) for c in range(8)]
    for (wt, r_wt, src) in ((wbg, r_wbg, wbg_d), (wbd, r_wbd, wbd_d), (wo, r_wo, wo_d)):
        loads += [(wt, r_wt, src, c * 256, c * 256) for c in range(4)]
    def emit_loads(k0, k1):
        for (wt, r_wt, src, c0, d0) in loads[k0:k1]:
            st, r_st = wst_r.next()
            sp.dma(st[:], src[:, c0:c0 + 256].rearrange("(j p) c -> p j c", p=128), writes=[r_st])
            pool.op(lambda e: e.tensor_copy(out=wt[:, :, d0:d0 + 256], in_=st[:]), reads=[r_st], writes=[r_wt])

    emit_loads(0, 8)
    hg_r = Ring([sb.t([128, 8, 512], BF16) for _ in range(2)])
    ag_r = Ring([sb.t([128, 8, 512], BF16) for _ in range(2)])
    bg_r = Ring([sb.t([128, 8, 512], BF16) for _ in range(2)])
    sig_r = Ring([sb.t([128, 16, 512], BF16) for _ in range(1)])
    yT_r = Ring([sb.t([128, 8, 512], BF16) for _ in range(1)])
    t1_r = Ring([sb.t([128, 512], F32) for _ in range(2)])
    t2_r = Ring([sb.t([128, 512], F32) for _ in range(2)])
    xt_r = Ring([sb.t([128, D], F32) for _ in range(3)])
    ot_r = Ring([sb.t([128, D], F32) for _ in range(2)])
    pring = Ring(banks, psum=True)
    r_out = Res()
    for g in range(OWN // 512):
        tsl = slice(g * 512, (g + 1) * 512)
        hg, r_hg = hg_r.next()
        ag, r_ag = ag_r.next()
        bg, r_bg = bg_r.next()
        sp.dma(hg[:], hT_s[:, :, CTX + g * 512:CTX + (g + 1) * 512].rearrange("j p t -> p j t"), reads=[r_hT], writes=[r_hg])
        sp.dma(ag[:], aT_s[:, :, tsl].rearrange("j p t -> p j t"), reads=[r_aT], writes=[r_ag])
        sp.dma(bg[:], bT_s[:, :, tsl].rearrange("j p t -> p j t"), reads=[r_bT], writes=[r_bg])
        sig, r_sig = sig_r.next()
        for m in range(16):
            ps, r_ps = pring.next()
            for j in range(8):
                pe.op(lambda e: e.matmul(ps[:, :], lhsT=wm[:, j, m * 128:(m + 1) * 128], rhs=hg[:, j, :],
                                         start=(j == 0), stop=(j == 7)), reads=[r_wm, r_hg], writes=[r_ps], inc=(j == 7))
            act.op(lambda e: e.activation(out=sig[:, m, :], in_=ps[:, :], func=AF.Sigmoid), reads=[r_ps], writes=[r_sig])
        if g == 0:
            emit_loads(8, 20)
        yT, r_yT = yT_r.next()
        for m in range(8):
            psA, r_psA = pring.next()
            psB, r_psB = pring.next()
            for j in range(8):
                pe.op(lambda e: e.matmul(psA[:, :], lhsT=wbg[:, j, m * 128:(m + 1) * 128], rhs=ag[:, j, :],
                                         start=(j == 0), stop=(j == 7)), reads=[r_wbg, r_ag], writes=[r_psA], inc=(j == 7))
            for j in range(8):
                pe.op(lambda e: e.matmul(psB[:, :], lhsT=wbd[:, j, m * 128:(m + 1) * 128], rhs=bg[:, j, :],
                                         start=(j == 0), stop=(j == 7)), reads=[r_wbd, r_bg], writes=[r_psB], inc=(j == 7))
            t1, r_t1 = t1_r.next()
            t2, r_t2 = t2_r.next()
            dve.op(lambda e: e.tensor_tensor(out=t1[:], in0=psA[:, :], in1=sig[:, m, :], op=ALU.mult),
                   reads=[r_psA, r_sig], writes=[r_t1])
            dve.op(lambda e: e.tensor_tensor(out=t2[:], in0=psB[:, :], in1=sig[:, 8 + m, :], op=ALU.mult),
                   reads=[r_psB, r_sig], writes=[r_t2])
            pool.op(lambda e: e.tensor_tensor(out=yT[:, m, :], in0=t1[:], in1=t2[:], op=ALU.add),
                    reads=[r_t1, r_t2], writes=[r_yT])
        for i in range(4):
            xt, r_xt = xt_r.next()
            row0 = g * 512 + i * 128
            sp.dma(xt[:], x_d[row0:row0 + 128, :], writes=[r_xt])
            ot, r_ot = ot_r.next()
            for cblk in range(2):
                csl = slice(cblk * 512, (cblk + 1) * 512)
                ps, r_ps = pring.next()
                for m in range(8):
                    pe.op(lambda e: e.matmul(ps[:, :], lhsT=yT[:, m, i * 128:(i + 1) * 128], rhs=wo[:, m, csl],
                                             start=(m == 0), stop=(m == 7)), reads=[r_wo, r_yT], writes=[r_ps], inc=(m == 7))
                t1, r_t1 = t1_r.next()
                dve.op(lambda e: e.tensor_tensor(out=t1[:], in0=ps[:, :], in1=gxb[:, csl], op=ALU.mult),
                       reads=[r_ps, r_gxb], writes=[r_t1])
                pool.op(lambda e: e.tensor_tensor(out=ot[:, csl], in0=t1[:], in1=xt[:, csl], op=ALU.add),
                        reads=[r_t1, r_xt], writes=[r_ot])
            pool.dma(out_d[row0:row0 + 128, :], ot[:], reads=[r_ot], writes=[r_out])
    fw.barrier()
    return nc, dbg_out


def _const_mats():
    m = np.zeros((NCB, 128, 128), np.float32)
    m[0] = np.eye(128)
    for mm in range(128):
        k = mm + 16 if (mm % 32) < 16 else mm - 16
        m[1][k, mm] = 1.0
    for k in range(128):
        for mm in range(128):
            if k // 64 == mm // 64:
                m[2][k, mm] = 1.0
    m[3][:] = 1.0
    j = np.arange(128)[:, None]
    i = np.arange(128)[None, :]
    sc = -1.0 / 16.0
    m[4] = sc * (j > i)
    m[5] = sc * (j < i)
    m[6] = sc * (j <= i)
    m[7] = sc * (j >= i)
    m[8] = sc * ((j <= 63).astype(np.float32) - (j <= i))
    m[9] = sc * ((j >= 64).astype(np.float32) - (j >= i))
    m[10] = (j <= i)
    m[11] = (j >= i)
    return m


def _rope_tables(positions):
    inv = 10000.0 ** (-np.arange(0, 32, 2, dtype=np.float64) / 32.0)
    row = (positions // 64).astype(np.float64)
    col = (positions % 64).astype(np.float64)
    cos = np.zeros((128, len(positions)), np.float64)
    sin = np.zeros((128, len(positions)), np.float64)
    for p in range(128):
        f = p % 64
        blk = f // 32
        i = f % 32
        ang = (row if blk == 0 else col) * inv[i % 16]
        cos[p] = np.cos(ang)
        sin[p] = (-1.0 if i < 16 else 1.0) * np.sin(ang)
    return cos, sin


def make_in_maps(x, c, ctx, c_ctx, w_ada, b_ada, w_in, gla_w_decay, gla_b_decay, gla_norm,
                 diff_q_norm, diff_k_norm, diff_lambda, diff_norm, w_br_gla, w_br_diff, w_out):
    f32 = np.float32
    x = np.asarray(x, f32)
    ctx = np.asarray(ctx, f32)
    c = np.asarray(c, f32)
    c_ctx = np.asarray(c_ctx, f32)
    w_ada0 = np.ascontiguousarray(np.asarray(w_ada, f32)[0])
    b_ada0 = np.asarray(b_ada, f32)[0]
    w_in0 = np.asarray(w_in, f32)[0]
    w_in_rev = w_in0.copy()
    w_in_rev[:, O_LR:O_LR + 16] = w_in0[:, O_LR + 16:O_LR + 32]
    w_in_rev[:, O_LR + 16:O_LR + 32] = w_in0[:, O_LR:O_LR + 16]
    wd = np.asarray(gla_w_decay, f32)[0]
    bd = np.asarray(gla_b_decay, f32)[0]
    wdec = np.zeros((2, 17, 512), f32)
    wdec[:, :16] = wd
    wdec[:, 16] = bd
    wdec_rev = np.ascontiguousarray(wdec[::-1])
    cbf = _const_mats().transpose(1, 0, 2).astype(NPBF)
    cbf = np.ascontiguousarray(cbf)
    common = {
        "w_ada": w_ada0,
        "b_ada": np.ascontiguousarray(b_ada0.reshape(24, 128).T),
        "b_ada_g": np.ascontiguousarray(b_ada0[2048:3072].reshape(1, D)),
        "gla_norm": np.ascontiguousarray(np.asarray(gla_norm, f32)[0].reshape(2, 128).T),
        "q_norm": np.ascontiguousarray(np.tile(np.asarray(diff_q_norm, f32)[0], 2).reshape(128, 1)),
        "k_norm": np.ascontiguousarray(np.tile(np.asarray(diff_k_norm, f32)[0], 2).reshape(128, 1)),
        "diff_norm": np.ascontiguousarray(np.asarray(diff_norm, f32)[0].reshape(128, 1)),
        "lam": np.ascontiguousarray(np.asarray(diff_lambda, f32)[0].reshape(1, 256)),
        "qk_row": np.ascontiguousarray(np.concatenate([np.asarray(diff_q_norm, f32)[0], np.asarray(diff_k_norm, f32)[0]]).reshape(1, 128)),
        "w_br_gla": np.ascontiguousarray(np.asarray(w_br_gla, f32)[0]),
        "w_br_diff": np.ascontiguousarray(np.asarray(w_br_diff, f32)[0]),
        "w_out": np.ascontiguousarray(np.asarray(w_out, f32)[0]),
        "cbf": cbf,
    }
    pos = np.arange(SEQ)
    tabs = []
    for rev in (False, True):
        p = pos[::-1] if rev else pos
        cs, sn = _rope_tables(p)
        cos = np.ones((128, NTOK), f32)
        sin = np.zeros((128, NTOK), f32)
        cos[:, CTX:] = cs
        sin[:, CTX:] = sn
        tabs.append((cos, sin))
    maps = []
    for core in range(8):
        b, half = core // 2, core % 2
        rev = half == 1
        xs = x[b][::-1] if rev else x[b]
        cx = ctx[b][::-1] if rev else ctx[b]
        cvec = np.stack([c[b].reshape(8, 128).T, c_ctx.reshape(8, 128).T], axis=-1)
        m = dict(common)
        m["x"] = np.ascontiguousarray(xs)
        m["ctx"] = np.ascontiguousarray(cx)
        m["cvec"] = np.ascontiguousarray(cvec.astype(f32))
        m["w_in"] = w_in_rev if rev else w_in0
        m["w_dec"] = wdec_rev if rev else wdec
        m["cos_t"], m["sin_t"] = tabs[1 if rev else 0]
        maps.append(m)
    return maps


def assemble(results):
    out = np.zeros((4, SEQ, D), np.float32)
    for core in range(8):
        b, half = core // 2, core % 2
        o = results[core]["out"]
        if half == 0:
            out[b, :OWN] = o
        else:
            out[b, OWN:] = o[::-1]
    return out


_NC_CACHE = {}


def kernel(**inputs):
    if "nc" not in _NC_CACHE:
        _NC_CACHE["nc"] = build_program()[0]
    nc = _NC_CACHE["nc"]
    maps = make_in_maps(**inputs)
    res = run_bass_kernel_spmd(nc, maps, core_ids=list(range(8)))
    return assemble(res.results)
```
